# Optimizing a Trainium2 kernel written in Bass

```python
import math
import jax, jax.numpy as jnp
from jax import lax
import numpy as np

D_MODEL = 2048
BATCH = 1
SEQ = 8192
DEPTH = 1
DEC_BATCH = 128
DEC_SEQ = 8
PAST_LEN = 16384
PAGE_SIZE = 128

N_META = 16
HEAD_DIM = 64
ATTN_WIDTH = D_MODEL // 2
N_HEADS = ATTN_WIDTH // HEAD_DIM
N_KV_HEADS = N_HEADS // 4
GQA_GROUP = N_HEADS // N_KV_HEADS
KV_WIDTH = N_KV_HEADS * HEAD_DIM
WINDOW = 128
BLOCK = 128
SSM_WIDTH = D_MODEL // 2
SSM_GROUP = 16
N_SSM_GROUPS = SSM_WIDTH // SSM_GROUP
SSM_STATE = 64
D_FF = 4 * D_MODEL
RMS_EPS = 1e-5
DT_MIN = 0.001
DT_MAX = 0.1
IN_COLS = ATTN_WIDTH + 2 * KV_WIDTH + SSM_WIDTH + 2 * D_MODEL
SPLITS = [ATTN_WIDTH, ATTN_WIDTH + KV_WIDTH, ATTN_WIDTH + 2 * KV_WIDTH, ATTN_WIDTH + 2 * KV_WIDTH + SSM_WIDTH]

kernel_name = "hybrid_swa_sink_s5_gated_step"

F32 = jnp.float32


def rmsnorm(x, g):
    xf = x.astype(F32)
    y = xf * lax.rsqrt(jnp.mean(xf * xf, axis=-1, keepdims=True) + RMS_EPS)
    return (y * g.astype(F32)).astype(x.dtype)


def alibi_slopes():
    return 2.0 ** (-8.0 * jnp.arange(1, N_HEADS + 1, dtype=F32) / N_HEADS)


def window_attend(q, k, v, q_pos, k_pos, sinks):
    s = jnp.einsum('...qkgd,...skd->...kgqs', q.astype(F32), k.astype(F32)) * (HEAD_DIM ** -0.5)
    dist = q_pos[..., :, None] - k_pos[..., None, :]
    valid = (dist >= 0) & (dist <= WINDOW) & (k_pos[..., None, :] >= 0)
    slopes = alibi_slopes().reshape(N_KV_HEADS, GQA_GROUP)[:, :, None, None]
    s = s - slopes * jnp.abs(dist).astype(F32)[..., None, None, :, :]
    s = jnp.where(valid[..., None, None, :, :], s, -jnp.inf)
    sink = sinks.astype(F32).reshape(N_KV_HEADS, GQA_GROUP)[:, :, None, None]
    m = jnp.maximum(jnp.max(s, axis=-1, keepdims=True), sink)
    p = jnp.exp(s - m)
    p = p / (jnp.sum(p, axis=-1, keepdims=True) + jnp.exp(sink - m))
    return jnp.einsum('...kgqs,...skd->...qkgd', p, v.astype(F32))


def ssm_discretize(a_re, a_im, log_dt, b_re, b_im):
    a_re = a_re.astype(F32); a_im = a_im.astype(F32)
    dt = jnp.exp(log_dt.astype(F32))[:, None]
    mag = jnp.exp(dt * a_re)
    ab_re = mag * jnp.cos(dt * a_im)
    ab_im = mag * jnp.sin(dt * a_im)
    nr = ab_re - 1.0
    ni = ab_im
    den = a_re * a_re + a_im * a_im
    fr = (nr * a_re + ni * a_im) / den
    fi = (ni * a_re - nr * a_im) / den
    b_re = b_re.astype(F32); b_im = b_im.astype(F32)
    bb_re = fr[..., None] * b_re - fi[..., None] * b_im
    bb_im = fr[..., None] * b_im + fi[..., None] * b_re
    return ab_re, ab_im, bb_re, bb_im


def ssm_scan(u, h0_re, h0_im, ab_re, ab_im, bb_re, bb_im, c_re, c_im, d):
    bu_re = jnp.einsum('gpc,ntgc->ntgp', bb_re, u)
    bu_im = jnp.einsum('gpc,ntgc->ntgp', bb_im, u)
    h0_re = h0_re.astype(F32); h0_im = h0_im.astype(F32)
    bu_re = bu_re.at[:, 0].add(ab_re * h0_re - ab_im * h0_im)
    bu_im = bu_im.at[:, 0].add(ab_re * h0_im + ab_im * h0_re)
    a_re = jnp.broadcast_to(ab_re, bu_re.shape)
    a_im = jnp.broadcast_to(ab_im, bu_im.shape)

    def combine(e1, e2):
        a1r, a1i, b1r, b1i = e1
        a2r, a2i, b2r, b2i = e2
        return (a2r * a1r - a2i * a1i,
                a2r * a1i + a2i * a1r,
                a2r * b1r - a2i * b1i + b2r,
                a2r * b1i + a2i * b1r + b2i)

    _, _, h_re, h_im = lax.associative_scan(combine, (a_re, a_im, bu_re, bu_im), axis=1)
    y = (jnp.einsum('gcp,ntgp->ntgc', c_re.astype(F32), h_re)
         - jnp.einsum('gcp,ntgp->ntgc', c_im.astype(F32), h_im)
         + d.astype(F32).reshape(N_SSM_GROUPS, SSM_GROUP) * u)
    return y, h_re[:, -1], h_im[:, -1]


def setup_inputs(seed: int = 0) -> dict:
    key = jax.random.key(seed)
    ks = jax.random.split(key, 32)
    n = lambda i, shape, scale: jax.random.normal(ks[i], shape, F32) * scale
    n_ar = jnp.arange(SSM_STATE, dtype=F32)
    return {
        "x_prompt": n(0, (BATCH, SEQ, D_MODEL), 1.0),
        "x_sample": n(1, (DEC_BATCH, DEC_SEQ, D_MODEL), 1.0),
        "cache_k": n(2, (DEC_BATCH, WINDOW, N_KV_HEADS, HEAD_DIM), 1.0),
        "cache_v": n(3, (DEC_BATCH, WINDOW, N_KV_HEADS, HEAD_DIM), 1.0),
        "state_ssm_re": n(4, (DEC_BATCH, N_SSM_GROUPS, SSM_STATE), 1.0),
        "state_ssm_im": n(5, (DEC_BATCH, N_SSM_GROUPS, SSM_STATE), 1.0),
        "meta_tokens": n(6, (N_META, D_MODEL), 1.0),
        "g_attn_norm": 1.0 + n(7, (D_MODEL,), 0.02),
        "w_in": n(8, (D_MODEL, IN_COLS), D_MODEL ** -0.5),
        "sinks": n(9, (N_HEADS,), 0.5),
        "ssm_a_re": -0.5 + n(10, (N_SSM_GROUPS, SSM_STATE), 0.01),
        "ssm_a_im": math.pi * n_ar[None, :] + n(11, (N_SSM_GROUPS, SSM_STATE), 0.01),
        "ssm_log_dt": jax.random.uniform(ks[12], (N_SSM_GROUPS,), F32, math.log(DT_MIN), math.log(DT_MAX)),
        "ssm_b_re": n(13, (N_SSM_GROUPS, SSM_STATE, SSM_GROUP), (2.0 * SSM_GROUP) ** -0.5),
        "ssm_b_im": n(14, (N_SSM_GROUPS, SSM_STATE, SSM_GROUP), (2.0 * SSM_GROUP) ** -0.5),
        "ssm_c_re": n(15, (N_SSM_GROUPS, SSM_GROUP, SSM_STATE), (2.0 * SSM_STATE) ** -0.5),
        "ssm_c_im": n(16, (N_SSM_GROUPS, SSM_GROUP, SSM_STATE), (2.0 * SSM_STATE) ** -0.5),
        "ssm_d": n(17, (SSM_WIDTH,), 1.0),
        "w_glu": n(18, (SSM_WIDTH, SSM_WIDTH), SSM_WIDTH ** -0.5),
        "b_glu": n(19, (SSM_WIDTH,), 0.02),
        "w_attn_branch": n(20, (ATTN_WIDTH, D_MODEL), ATTN_WIDTH ** -0.5),
        "w_ssm_branch": n(21, (SSM_WIDTH, D_MODEL), SSM_WIDTH ** -0.5),
        "w_out": n(22, (D_MODEL, D_MODEL), D_MODEL ** -0.5),
        "g_mlp_norm": 1.0 + n(23, (D_MODEL,), 0.02),
        "w_up": n(24, (D_MODEL, D_FF), D_MODEL ** -0.5),
        "w_down": n(25, (D_FF, D_MODEL), D_FF ** -0.5),
        "g_final_norm": 1.0 + n(26, (D_MODEL,), 0.02),
    }


def reference(x_prompt, x_sample, cache_k, cache_v, state_ssm_re, state_ssm_im,
              meta_tokens, g_attn_norm, w_in, sinks, ssm_a_re, ssm_a_im, ssm_log_dt,
              ssm_b_re, ssm_b_im, ssm_c_re, ssm_c_im, ssm_d, w_glu, b_glu,
              w_attn_branch, w_ssm_branch, w_out, g_mlp_norm, w_up, w_down, g_final_norm):
    ab_re, ab_im, bb_re, bb_im = ssm_discretize(ssm_a_re, ssm_a_im, ssm_log_dt, ssm_b_re, ssm_b_im)

    def layer(x, attend, h0_re, h0_im):
        nb, t, _ = x.shape
        h = rmsnorm(x, g_attn_norm)
        proj = h @ w_in
        q, k, v, u, gates = jnp.split(proj, SPLITS, axis=-1)
        q = q.reshape(nb, t, N_KV_HEADS, GQA_GROUP, HEAD_DIM)
        k = k.reshape(nb, t, N_KV_HEADS, HEAD_DIM)
        v = v.reshape(nb, t, N_KV_HEADS, HEAD_DIM)
        a = attend(q, k, v)
        us = u.astype(F32).reshape(nb, t, N_SSM_GROUPS, SSM_GROUP)
        ys, hr, hi = ssm_scan(us, h0_re, h0_im, ab_re, ab_im, bb_re, bb_im, ssm_c_re, ssm_c_im, ssm_d)
        z = jax.nn.gelu(ys.reshape(nb, t, SSM_WIDTH))
        s = z * jax.nn.sigmoid(z @ w_glu.astype(F32) + b_glu.astype(F32))
        g_a, g_s = jnp.split(jax.nn.sigmoid(gates.astype(F32)), 2, axis=-1)
        merged = g_a * (a @ w_attn_branch.astype(F32)) + g_s * (s @ w_ssm_branch.astype(F32))
        x = x + merged.astype(x.dtype) @ w_out
        hm = rmsnorm(x, g_mlp_norm)
        x = x + jnp.square(jax.nn.relu(hm @ w_up)) @ w_down
        return x, k, v, hr, hi

    def attend_prompt(q, k, v):
        nb, t = q.shape[:2]
        pad = (-t) % BLOCK
        n_blk = (t + pad) // BLOCK
        padf = lambda arr: jnp.pad(arr, ((0, 0), (pad, 0)) + ((0, 0),) * (arr.ndim - 2))
        qb = padf(q).reshape(nb, n_blk, BLOCK, N_KV_HEADS, GQA_GROUP, HEAD_DIM)
        kb = padf(k).reshape(nb, n_blk, BLOCK, N_KV_HEADS, HEAD_DIM)
        vb = padf(v).reshape(nb, n_blk, BLOCK, N_KV_HEADS, HEAD_DIM)
        with_prev = lambda arr: jnp.concatenate(
            [jnp.concatenate([jnp.zeros_like(arr[:, :1]), arr[:, :-1]], axis=1), arr], axis=2)
        pos = (jnp.arange(n_blk * BLOCK, dtype=jnp.int32) - pad).reshape(n_blk, BLOCK)
        k_pos = jnp.concatenate([pos - BLOCK, pos], axis=-1)
        o = window_attend(qb, with_prev(kb), with_prev(vb), pos, k_pos, sinks)
        return o.reshape(nb, n_blk * BLOCK, ATTN_WIDTH)[:, pad:]

    def attend_sample(q, k, v):
        nb, t = q.shape[:2]
        w = cache_k.shape[1]
        kk = jnp.concatenate([cache_k.astype(k.dtype), k], axis=1)
        vv = jnp.concatenate([cache_v.astype(v.dtype), v], axis=1)
        q_pos = PAST_LEN + jnp.arange(t, dtype=jnp.int32)
        k_pos = jnp.concatenate([PAST_LEN - w + jnp.arange(w, dtype=jnp.int32), q_pos])
        o = window_attend(q, kk, vv, q_pos, k_pos, sinks)
        return o.reshape(nb, t, ATTN_WIDTH)

    bp = x_prompt.shape[0]
    meta = jnp.broadcast_to(meta_tokens.astype(x_prompt.dtype)[None], (bp, N_META, D_MODEL))
    xp = jnp.concatenate([meta, x_prompt], axis=1)
    zeros_state = jnp.zeros((bp, N_SSM_GROUPS, SSM_STATE), F32)
    for _ in range(DEPTH):
        xp, kp, vp, hrp, hip = layer(xp, attend_prompt, zeros_state, zeros_state)
    y_prompt = rmsnorm(xp, g_final_norm)[:, N_META:]
    k_prompt = kp[:, -WINDOW:]
    v_prompt = vp[:, -WINDOW:]
    ssm_re_prompt = hrp.astype(state_ssm_re.dtype)
    ssm_im_prompt = hip.astype(state_ssm_im.dtype)

    xs = x_sample
    for _ in range(DEPTH):
        xs, ks_new, vs_new, hrs, his = layer(xs, attend_sample, state_ssm_re, state_ssm_im)
    y_sample = rmsnorm(xs, g_final_norm)
    k_sample = jnp.concatenate([cache_k.astype(ks_new.dtype), ks_new], axis=1)[:, -WINDOW:]
    v_sample = jnp.concatenate([cache_v.astype(vs_new.dtype), vs_new], axis=1)[:, -WINDOW:]
    ssm_re_sample = hrs.astype(state_ssm_re.dtype)
    ssm_im_sample = his.astype(state_ssm_im.dtype)

    return (y_prompt, y_sample, k_prompt, v_prompt, ssm_re_prompt, ssm_im_prompt,
            k_sample, v_sample, ssm_re_sample, ssm_im_sample)
```

```python
import contextlib
import math
import numpy as np
import ml_dtypes
import concourse.bass as bass
import concourse.mybir as mybir
from concourse.bass_utils import run_bass_kernel_spmd

F32 = mybir.dt.float32
BF16 = mybir.dt.bfloat16
I32 = mybir.dt.int32
ALU = mybir.AluOpType
AF = mybir.ActivationFunctionType
AX = mybir.AxisListType
PE, ACT, DVE, POOL, SP = "tensor", "scalar", "vector", "gpsimd", "sync"
COMPUTE = (PE, ACT, DVE, POOL)
TWO_PI = 2.0 * math.pi

NCORES = 8
NPRE = 57
NOWN = 9
GROUPS = [(0, 1, 2), (3, 4, 5), (6, 7, 8)]
NSLOT = 4
LOOKAHEAD = 2


class Buf:
    __slots__ = ("name", "last_write", "reads", "excl")

    def __init__(self, name, hazards=(), excl=False):
        self.name = name
        self.excl = excl
        self.last_write = None
        self.reads = {}
        for ev in hazards:
            self.add_read(ev)

    def add_read(self, ev):
        k = (ev[0], ev[1])
        if k not in self.reads or self.reads[k][2] < ev[2]:
            self.reads[k] = ev

    def events(self):
        ev = list(self.reads.values())
        if self.last_write is not None:
            ev.append(self.last_write)
        return ev


class Op:
    __slots__ = ("eng", "fn", "deps", "is_dma", "dsem", "dval", "idx", "signal")


class Prog:
    def __init__(self, nc, same_engine_sync=True):
        self.nc = nc
        self.same_engine_sync = same_engine_sync
        self.streams = {e: [] for e in (PE, ACT, DVE, POOL, SP)}
        self.dma_sems = {}
        self.n_ops = 0
        self.n_waits = 0
        self.waits_by = {}

    def _collect(self, reads, writes, eng=None):
        deps = []
        for b in reads:
            if b.last_write is not None:
                deps.append(b.last_write)
            if b.excl:
                deps.extend(ev for k, ev in b.reads.items() if not (k[0] == "E" and k[1] == eng))
        for b in writes:
            if b.last_write is not None:
                deps.append(b.last_write)
            deps.extend(b.reads.values())
        return deps

    def _commit(self, ev, reads, writes):
        for b in reads:
            b.add_read(ev)
        for b in writes:
            b.last_write = ev
            b.reads = {}

    def op(self, eng, fn, reads=(), writes=()):
        o = Op()
        o.eng, o.fn, o.is_dma, o.signal = eng, fn, False, False
        o.deps = self._collect(reads, writes, eng)
        st = self.streams[eng]
        o.idx = len(st)
        st.append(o)
        self._commit(("E", eng, o.idx), reads, writes)
        self.n_ops += 1
        return o

    def dma(self, queue, fn, reads=(), writes=(), sem=None):
        o = Op()
        o.eng, o.fn, o.is_dma, o.signal = queue, fn, True, False
        key = sem if sem is not None else (writes[0].name if writes else reads[0].name)
        o.deps = [d for d in self._collect(reads, writes) if not (d[0] == "D" and d[1] == key)]
        cnt = self.dma_sems.get(key, 0) + 16
        self.dma_sems[key] = cnt
        o.dsem, o.dval = key, cnt
        st = self.streams[queue]
        o.idx = len(st)
        st.append(o)
        self._commit(("D", key, cnt), reads, writes)
        self.n_ops += 1
        return o

    def emit(self, final_bufs=()):
        nc = self.nc
        fin = Op()
        fin.eng, fin.fn, fin.is_dma, fin.signal = SP, None, False, False
        fin.deps = []
        for b in final_bufs:
            fin.deps.extend(b.events())
        fin.idx = len(self.streams[SP])
        self.streams[SP].append(fin)
        for e, st in self.streams.items():
            for o in st:
                for d in o.deps:
                    if d[0] == "E" and not (d[1] == e and (e == PE or not self.same_engine_sync)):
                        self.streams[d[1]][d[2]].signal = True
        val = {}
        for e in COMPUTE:
            c, v = 0, []
            for o in self.streams[e]:
                if o.signal:
                    c += 1
                v.append(c)
            val[e] = v
        self.maxvals = {e: (val[e][-1] if val[e] else 0) for e in COMPUTE}
        stack = contextlib.ExitStack()
        esem = {e: stack.enter_context(nc.semaphore("es_" + e)) for e in COMPUTE}
        dsem = {k: stack.enter_context(nc.semaphore("ds_%d" % i)) for i, k in enumerate(self.dma_sems)}
        block = stack.enter_context(nc.Block())

        def make(e):
            st = self.streams[e]

            def body(eng):
                waited = {}
                for o in st:
                    need = {}
                    for d in o.deps:
                        if d[0] == "E":
                            if d[1] == e and (e == PE or not self.same_engine_sync):
                                continue
                            s, v = esem[d[1]], val[d[1]][d[2]]
                        else:
                            s, v = dsem[d[1]], d[2]
                        k = id(s)
                        if v > need.get(k, (None, 0))[1]:
                            need[k] = (s, v)
                    for k, (s, v) in need.items():
                        if v > waited.get(k, 0):
                            eng.wait_ge(s, v)
                            waited[k] = v
                            self.n_waits += 1
                            self.waits_by[e] = self.waits_by.get(e, 0) + 1
                    if o.fn is None:
                        continue
                    ins = o.fn(eng)
                    if o.is_dma:
                        ins.then_inc(dsem[o.dsem], 16)
                    elif o.signal:
                        ins.then_inc(esem[e], 1)
            return body

        for e in (PE, ACT, DVE, POOL, SP):
            if self.streams[e]:
                getattr(block, e)(make(e))
        stack.close()


def _host_consts():
    H = 16
    slopes = 2.0 ** (-8.0 * np.arange(1, H + 1, dtype=np.float64) / H)
    hperm = [4 * kv + i for kv in range(4) for i in (0, 2, 1, 3)]
    sl = slopes[hperm][None, :, None]
    k = np.arange(128)[:, None, None].astype(np.float64)
    q = np.arange(128)[None, None, :].astype(np.float64)
    e_cur = np.where(k <= q, np.exp(-sl * (q - k)), 0.0)
    e_prev = np.where(k >= q, np.exp(-sl * (q + 128 - k)), 0.0)
    e_prev0 = e_prev.copy()
    e_prev0[:112] = 0.0
    kb, ki = np.arange(128)[:, None, None] // 8, np.arange(128)[:, None, None] % 8
    qb, qi = np.arange(128)[None, None, :] // 8, np.arange(128)[None, None, :] % 8
    e_new = np.where((kb == qb) & (ki <= qi), np.exp(-sl * (qi - ki)), 0.0)
    w = np.arange(128)[:, None, None].astype(np.float64)
    i8 = np.arange(8)[None, None, :].astype(np.float64)
    e_cache = np.where(w >= i8, np.exp(-sl * (128 + i8 - w)), 0.0)
    bf = ml_dtypes.bfloat16
    maskc = (np.arange(128)[:, None] // 32 == np.arange(4)[None, :]).astype(np.float32)
    return dict(
        e_cur=e_cur.astype(np.float32).astype(bf), e_prev=e_prev.astype(np.float32).astype(bf),
        e_prev0=e_prev0.astype(np.float32).astype(bf), e_new=e_new.astype(np.float32).astype(bf),
        e_cache=e_cache.astype(np.float32).astype(bf),
        ident_bf=np.eye(128, dtype=np.float32).astype(bf), ident_f=np.eye(128, dtype=np.float32),
        kcol=(127 - np.arange(128, dtype=np.float32)).reshape(128, 1), maskc=maskc, hperm=hperm)


def build_program(npre=NPRE, dbg=None):
    dbg = dbg or {}
    nc = bass.Bass("TRN2", target_bir_lowering=False)
    es = contextlib.ExitStack()
    P = Prog(nc, same_engine_sync=not dbg.get("no_ses"))
    hazards = []

    def din(name, shape, dt=F32):
        return nc.dram_tensor(name, list(shape), dt, kind="ExternalInput").ap()

    def dout(name, shape, dt=F32):
        return nc.dram_tensor(name, list(shape), dt, kind="ExternalOutput").ap()

    class T:
        def __init__(self, stack, name, shape, dt=F32):
            self.t = stack.enter_context(nc.sbuf_tensor("s_" + name, list(shape), dt))
            self.b = Buf(name, hazards)

        def __getitem__(self, k):
            return self.t[k]

    def free_scope(stack, tensors):
        for t in tensors:
            hazards.extend(t.b.events())
        stack.close()

    def toks(xs):
        return [x.b if hasattr(x, "b") else x for x in xs]

    def tt(eng, out, in0, in1, op, R, W):
        P.op(eng, lambda e: e.tensor_tensor(out=out, in0=in0, in1=in1, op=op), toks(R), toks(W))

    def ts(eng, out, in0, s1, s2, op0, op1, R, W):
        if s2 is None:
            P.op(eng, lambda e: e.tensor_scalar(out=out, in0=in0, scalar1=s1, scalar2=None, op0=op0), toks(R), toks(W))
        else:
            P.op(eng, lambda e: e.tensor_scalar(out=out, in0=in0, scalar1=s1, scalar2=s2, op0=op0, op1=op1), toks(R), toks(W))

    def stt(eng, out, in0, scalar, in1, op0, op1, R, W):
        P.op(eng, lambda e: e.scalar_tensor_tensor(out=out, in0=in0, scalar=scalar, in1=in1, op0=op0, op1=op1), toks(R), toks(W))

    def cp(eng, out, in_, R, W):
        if eng == ACT:
            P.op(eng, lambda e: e.copy(out=out, in_=in_), toks(R), toks(W))
        else:
            P.op(eng, lambda e: e.tensor_copy(out=out, in_=in_), toks(R), toks(W))

    def act(out, in_, func, R, W, bias=None, scale=None, accum=None):
        kw = {}
        if bias is not None:
            kw["bias"] = bias
        if scale is not None:
            kw["scale"] = scale
        if accum is not None:
            kw["accum_out"] = accum
        P.op(ACT, lambda e: e.activation(out=out, in_=in_, func=func, **kw), toks(R), toks(W))

    def mm(out, lhsT, rhs, start, stop, R, W, tp=None):
        if tp is None:
            P.op(PE, lambda e: e.matmul(out, lhsT=lhsT, rhs=rhs, start=start, stop=stop), toks(R), toks(W))
        else:
            P.op(PE, lambda e: e.matmul(out, lhsT=lhsT, rhs=rhs, start=start, stop=stop, tile_position=tp), toks(R), toks(W))

    def tr(out, in_, idn, R, W):
        P.op(PE, lambda e: e.transpose(out=out, in_=in_, identity=idn), toks(R), toks(W))

    def dma(q, out, in_, R, W, sem, slow=False):
        if slow:
            P.dma(q, lambda e: e.dma_start(out=out, in_=in_, allow_slow_non_contiguous=True), toks(R), toks(W), sem=sem)
        else:
            P.dma(q, lambda e: e.dma_start(out=out, in_=in_), toks(R), toks(W), sem=sem)

    def memset(eng, out, v, W):
        P.op(eng, lambda e: e.memset(out, v), [], toks(W))

    xprev = din("xprev", [npre * 128, 2048])
    xown = din("xown", [NOWN * 128, 2048])
    cache_k = din("cache_k", [16, 128, 256]); cache_v = din("cache_v", [16, 128, 256])
    st_re = din("st_re", [16, 4096]); st_im = din("st_im", [16, 4096])
    g_attn = din("g_attn_norm", [2048]); g_mlp = din("g_mlp_norm", [2048]); g_fin = din("g_final_norm", [2048])
    w_in = din("w_in", [2048, 6656]); sinks_p = din("sinks_perm", [16])
    a_re = din("ssm_a_re", [64, 64]); a_im = din("ssm_a_im", [64, 64]); log_dt = din("ssm_log_dt", [64])
    b_re = din("ssm_b_re", [64, 64, 16]); b_im = din("ssm_b_im", [64, 64, 16])
    c_re = din("ssm_c_re", [64, 16, 64]); c_im = din("ssm_c_im", [64, 16, 64])
    ssm_d = din("ssm_d", [1024]); w_glu = din("w_glu", [1024, 1024]); b_glu = din("b_glu", [1024])
    w_ab = din("w_attn_branch", [1024, 2048]); w_sb = din("w_ssm_branch", [1024, 2048])
    w_out = din("w_out", [2048, 2048]); w_up = din("w_up", [2048, 8192]); w_down = din("w_down", [8192, 2048])
    ident_bf_d = din("ident_bf", [128, 128], BF16); ident_f_d = din("ident_f", [128, 128])
    kcol_d = din("kcol", [128, 1]); maskc_d = din("maskc", [128, 4]); padm_d = din("padmask", [128, 1])
    e_cur_d = din("e_cur", [128, 16, 128], BF16); e_prev_d = din("e_prev", [128, 16, 128], BF16)
    e_new_d = din("e_new", [128, 16, 128], BF16); e_cache_d = din("e_cache", [128, 16, 8], BF16)

    y_own = dout("y_own", [NOWN * 128, 2048])
    kv_last = dout("kv_last", [2, 128, 256])
    ksamp = dout("ksamp", [16, 128, 256]); vsamp = dout("vsamp", [16, 128, 256])
    ssm_fin = dout("ssm_fin", [128, 2, 32])
    ssm_samp = dout("ssm_samp", [16, 2, 4096])
    ca_scr = nc.dram_tensor("ca_scr", [8, 128, 4 * 17 * 2 * 32], BF16, kind="Internal").ap()
    dbg_out = {k: dout("dbg_" + k, shp, dt) for k, (shp, dt) in dbg.items()}
    final_toks = []

    banks = [es.enter_context(nc.psum_tensor("bank%d" % i, [128, 512], F32)) for i in range(8)]
    bkb = [Buf("bank%d" % i, excl=True) for i in range(8)]

    ident = T(es, "ident", [128, 128], BF16); identf = T(es, "identf", [128, 128])
    negpi = T(es, "negpi", [128, 1]); epsc = T(es, "epsc", [128, 1]); ones_bf = T(es, "ones", [128, 64], BF16)
    gcolA = T(es, "gcolA", [128, 16]); gcolM = T(es, "gcolM", [128, 16]); gfin = T(es, "gfin", [128, 2048])
    Dcol = T(es, "Dcol", [128, 8]); bglu = T(es, "bglu", [128, 8]); esink = T(es, "esink", [128, 16])
    maskc = T(es, "maskc", [128, 4]); kcol = T(es, "kcol", [128, 1]); padm = T(es, "padm", [128, 1])
    Ecur = T(es, "Ecur", [128, 16, 128], BF16); Eprev = T(es, "Eprev", [128, 16, 128], BF16)
    Enew = T(es, "Enew", [128, 16, 128], BF16); Ecache = T(es, "Ecache", [128, 16, 8], BF16)
    zT = T(es, "zT", [128, 8, 1152], BF16)

    dma(SP, ident[:], ident_bf_d, [], [ident], None)
    dma(SP, kcol[:], kcol_d, [], [kcol], None)
    dma(SP, gcolA[:], g_attn.rearrange("(k p) -> p k", p=128), [], [gcolA], None, slow=True)
    onesc = T(es, "onesc", [128, 1]); memset(DVE, onesc[:], 1.0, [onesc])
    memset(DVE, negpi[:], -math.pi, [negpi]); memset(DVE, epsc[:], 1e-5, [epsc]); memset(DVE, ones_bf[:], 1.0, [ones_bf])

    def sin_reduce(th, thi, b_th, b_thi):
        ts(DVE, thi, th, 1.0 / TWO_PI, None, ALU.mult, None, [b_th], [b_thi])
        thf = thi.bitcast(F32)
        cp(DVE, thf, thi, [b_thi], [b_thi])
        stt(DVE, th, thf, -TWO_PI, th, ALU.mult, ALU.add, [b_thi, b_th], [b_th])
        ts(DVE, th, th, -3.1415925, 3.1415925, ALU.max, ALU.min, [b_th], [b_th])
        act(th, th, AF.Sin, [b_th], [b_th])

    def rms_stats(x_ap, junk_ap, ss, rstd, R, W_junk):
        act(junk_ap, x_ap, AF.Square, R, [W_junk, ss], accum=ss[:])
        act(rstd[:], ss[:], AF.Ln, [ss, epsc], [rstd], bias=epsc[:], scale=1.0 / 2048)
        act(rstd[:], rstd[:], AF.Exp, [rstd], [rstd], scale=-0.5)

    sA = contextlib.ExitStack()
    KS = list(range(17)) + [128, -8]
    NK = len(KS)
    aH = T(sA, "aH", [128, 2, 32]); ldtH = T(sA, "ldtH", [128, 32]); dtH = T(sA, "dtH", [128, 32])
    lamH = T(sA, "lamH", [128, 2, 32]); AK = T(sA, "AK", [128, NK, 2, 32])
    bH = T(sA, "bH", [128, 2, 32, 16]); cH = T(sA, "cH", [128, 2, 32, 16])
    BbH = T(sA, "BbH", [128, 2, 32, 16]); fH = T(sA, "fH", [128, 2, 32])
    ApowT = T(sA, "ApowT", [128, 32, 2, 128], BF16)
    Hst = T(sA, "Hst", [128, 2, 32])
    BbBD = T(sA, "BbBD", [128, 32, 2, 32], BF16)
    for two in range(2):
        ps = slice(64 * two, 64 * two + 64)
        for ri, src in enumerate((a_re, a_im)):
            dma(SP, aH[ps, ri, :], src.rearrange("(pair two) p -> two p pair", two=2)[two], [], [aH], None, slow=True)
        dma(SP, ldtH[ps, :], bass.AP(tensor=log_dt.tensor, offset=two, ap=[[0, 64], [2, 32]]), [], [ldtH], None, slow=True)
        for ri, src in enumerate((b_re, b_im)):
            dma(SP, bH[ps, ri, :, :], src.rearrange("(pair two) p c -> two p pair c", two=2)[two], [], [bH], None)
    act(dtH[:], ldtH[:], AF.Exp, [ldtH], [dtH])
    tt(DVE, lamH[:], aH[:], dtH[:, None, :].broadcast_to([128, 2, 32]), ALU.mult, [aH, dtH], [lamH])
    sA1 = contextlib.ExitStack()
    mag = T(sA1, "mag", [128, NK, 32]); ang = T(sA1, "ang", [128, NK, 2, 32]); angi = T(sA1, "angi", [128, NK, 2, 32], I32)
    tmpa = T(sA1, "tmpa", [128, 4, 32]); tb = T(sA1, "tb", [128, 2, 32, 16])
    big0 = T(sA1, "big0", [128, 4096]); big1 = T(sA1, "big1", [128, 4096]); big2 = T(sA1, "big2", [128, 4096])
    bigI = T(sA1, "bigI", [128, 4096], I32); dtT = T(sA1, "dtT", [128, 64])
    for i, k in enumerate(KS):
        act(mag[:, i, :], lamH[:, 0, :], AF.Exp, [lamH], [mag], scale=float(k))
        ts(DVE, ang[:, i, 0, :], lamH[:, 1, :], float(k), math.pi / 2, ALU.mult, ALU.add, [lamH], [ang])
        ts(DVE, ang[:, i, 1, :], lamH[:, 1, :], float(k), None, ALU.mult, None, [lamH], [ang])
    sin_reduce(ang[:], angi[:], ang.b, angi.b)
    tt(DVE, AK[:], ang[:], mag[:, :, None, :].broadcast_to([128, NK, 2, 32]), ALU.mult, [ang, mag], [AK])
    i1 = KS.index(1)
    ts(DVE, tmpa[:, 0, :], AK[:, i1, 0, :], -1.0, None, ALU.add, None, [AK], [tmpa])
    tt(DVE, tmpa[:, 1, :], aH[:, 0, :], aH[:, 0, :], ALU.mult, [aH], [tmpa])
    tt(DVE, tmpa[:, 2, :], aH[:, 1, :], aH[:, 1, :], ALU.mult, [aH], [tmpa])
    tt(DVE, tmpa[:, 1, :], tmpa[:, 1, :], tmpa[:, 2, :], ALU.add, [tmpa], [tmpa])
    P.op(DVE, lambda e: e.reciprocal(out=tmpa[:, 1, :], in_=tmpa[:, 1, :]), [tmpa.b], [tmpa.b])
    tt(DVE, tmpa[:, 2, :], tmpa[:, 0, :], aH[:, 0, :], ALU.mult, [tmpa, aH], [tmpa])
    tt(DVE, tmpa[:, 3, :], AK[:, i1, 1, :], aH[:, 1, :], ALU.mult, [AK, aH], [tmpa])
    tt(DVE, tmpa[:, 2, :], tmpa[:, 2, :], tmpa[:, 3, :], ALU.add, [tmpa], [tmpa])
    tt(DVE, fH[:, 0, :], tmpa[:, 2, :], tmpa[:, 1, :], ALU.mult, [tmpa], [fH])
    tt(DVE, tmpa[:, 2, :], AK[:, i1, 1, :], aH[:, 0, :], ALU.mult, [AK, aH], [tmpa])
    tt(DVE, tmpa[:, 3, :], tmpa[:, 0, :], aH[:, 1, :], ALU.mult, [tmpa, aH], [tmpa])
    tt(DVE, tmpa[:, 2, :], tmpa[:, 2, :], tmpa[:, 3, :], ALU.subtract, [tmpa], [tmpa])
    tt(DVE, fH[:, 1, :], tmpa[:, 2, :], tmpa[:, 1, :], ALU.mult, [tmpa], [fH])
    frb = fH[:, 0, :, None].broadcast_to([128, 32, 16]); fib = fH[:, 1, :, None].broadcast_to([128, 32, 16])
    tt(DVE, tb[:, 0], bH[:, 0], frb, ALU.mult, [bH, fH], [tb]); tt(DVE, tb[:, 1], bH[:, 1], fib, ALU.mult, [bH, fH], [tb])
    tt(DVE, BbH[:, 0], tb[:, 0], tb[:, 1], ALU.subtract, [tb], [BbH])
    tt(DVE, tb[:, 0], bH[:, 1], frb, ALU.mult, [bH, fH], [tb]); tt(DVE, tb[:, 1], bH[:, 0], fib, ALU.mult, [bH, fH], [tb])
    tt(DVE, BbH[:, 1], tb[:, 0], tb[:, 1], ALU.add, [tb], [BbH])
    memset(DVE, BbBD[:], 0.0, [BbBD])
    for half in range(2):
        for ri in range(2):
            cp(DVE, BbBD[64 * half:64 * half + 64, :, ri, 16 * half:16 * half + 16], BbH[64 * half:64 * half + 64, ri, :, :], [BbH], [BbBD])
    dma(SP, big0[:], a_re.rearrange("g p -> (g p)").partition_broadcast(128), [], [big0], "big0")
    dma(SP, big1[:], a_im.rearrange("g p -> (g p)").partition_broadcast(128), [], [big1], "big1")
    dma(SP, dtT[:], log_dt.partition_broadcast(128), [], [dtT], "big2")
    dma(SP, identf[:], ident_f_d, [], [identf], None)
    dma(SP, maskc[:], maskc_d, [], [maskc], None); dma(SP, padm[:], padm_d, [], [padm], None)
    dma(SP, gcolM[:], g_mlp.rearrange("(k p) -> p k", p=128), [], [gcolM], None, slow=True)
    dma(SP, Dcol[:], ssm_d.rearrange("(k p) -> p k", p=128), [], [Dcol], None, slow=True)
    dma(SP, bglu[:], b_glu.rearrange("(k p) -> p k", p=128), [], [bglu], None, slow=True)
    dma(SP, gfin[:], g_fin.partition_broadcast(128), [], [gfin], None)
    dma(SP, esink[:], sinks_p.partition_broadcast(128), [], [esink], None)
    dma(SP, Ecur[:], e_cur_d, [], [Ecur], None); dma(SP, Eprev[:], e_prev_d, [], [Eprev], None)
    dma(SP, Enew[:], e_new_d, [], [Enew], None); dma(SP, Ecache[:], e_cache_d, [], [Ecache], None)
    act(esink[:], esink[:], AF.Exp, [esink], [esink])
    for two in range(2):
        ps = slice(64 * two, 64 * two + 64)
        for ri, src in enumerate((c_re, c_im)):
            v = src.rearrange("(pair two) c p -> two p pair c", two=2)[two]
            for pr in range(32):
                dma(SP, cH[ps, ri, pr, :], v[:, pr, :], [], [cH], None, slow=True)
    act(dtT[:], dtT[:], AF.Exp, [dtT], [dtT])
    dtb = dtT[:, :, None].broadcast_to([128, 64, 64])
    tt(DVE, big0[:].rearrange("s (g p) -> s g p", p=64), big0[:].rearrange("s (g p) -> s g p", p=64), dtb, ALU.mult, [big0, dtT], [big0])
    tt(DVE, big1[:].rearrange("s (g p) -> s g p", p=64), big1[:].rearrange("s (g p) -> s g p", p=64), dtb, ALU.mult, [big1, dtT], [big1])
    act(big0[:], big0[:], AF.Exp, [big0, kcol], [big0], scale=kcol[:])
    for ri, off in ((0, math.pi / 2), (1, 0.0)):
        ts(DVE, big2[:], big1[:], kcol[:], off, ALU.mult, ALU.add, [big1, kcol], [big2])
        sin_reduce(big2[:], bigI[:], big2.b, bigI.b)
        tt(DVE, ApowT[:, :, ri, :], big2[:].rearrange("s (a q) -> s a q", q=128), big0[:].rearrange("s (a q) -> s a q", q=128), ALU.mult, [big2, big0], [ApowT])
    free_scope(sA1, [mag, ang, angi, tmpa, tb, big0, big1, big2, bigI, dtT])

    NJ = 80
    sU = contextlib.ExitStack()
    uTd = T(sU, "uTd", [128, 8, 16, NJ], BF16)
    sBC = contextlib.ExitStack()
    Wu = T(sBC, "Wu", [128, 16, 1024], BF16)
    for k4 in range(4):
        dma(POOL, Wu[:, 4 * k4:4 * k4 + 4, :], w_in[512 * k4:512 * (k4 + 1), 1536:2560].rearrange("(k p) c -> p k c", p=128), [], [Wu], "Wu")
    for k in range(16):
        ts(DVE, Wu[:, k, :], Wu[:, k, :], gcolA[:, k:k + 1], None, ALU.mult, None, [Wu, gcolA], [Wu])
    sB = contextlib.ExitStack()
    T1 = T(sB, "T1", [128, 32, 2, 16]); T2 = T(sB, "T2", [128, 32, 2, 16])
    cp(DVE, T1[:, :, 0, :], BbH[:, 0], [BbH], [T1]); ts(DVE, T1[:, :, 1, :], BbH[:, 1], -1.0, None, ALU.mult, None, [BbH], [T1])
    cp(DVE, T2[:, :, 0, :], BbH[:, 1], [BbH], [T2]); cp(DVE, T2[:, :, 1, :], BbH[:, 0], [BbH], [T2])
    xb = [T(sB, "xb%d" % i, [128, 2048], BF16) for i in range(3)]
    junk = T(sB, "junk", [128, 2048], BF16)
    ssq = [T(sB, "ss%d" % i, [128, 1]) for i in range(3)]; rstd = [T(sB, "rstd%d" % i, [128, 1]) for i in range(3)]
    xT = [T(sB, "xT%d" % i, [128, 16, 128], BF16) for i in range(3)]
    usb = [T(sB, "usb%d" % i, [128, 2, 1024], BF16) for i in range(2)]
    prod = T(sB, "prod", [128, 2, 32, 2, 2, 16]); Stile = [T(sB, "Stile%d" % i, [128, 2, 32]) for i in range(2)]
    m12 = T(sB, "m12", [128, 2, 2, 32]); e12 = T(sB, "e12", [128, 2, 32])
    memset(POOL, Hst[:], 0.0, [Hst])

    def cstep(eng, H, Pidx, S_ap, S_tok, m12, e12):
        Pr = AK[:, Pidx, 0, :]; Pi = AK[:, Pidx, 1, :]
        tt(eng, m12[:, 0], H[:], Pr[:, None, :].broadcast_to([128, 2, 32]), ALU.mult, [H, AK], [m12])
        tt(eng, m12[:, 1], H[:], Pi[:, None, :].broadcast_to([128, 2, 32]), ALU.mult, [H, AK], [m12])
        tt(eng, e12[:, 0, :], m12[:, 0, 0, :], m12[:, 1, 1, :], ALU.subtract, [m12], [e12])
        tt(eng, e12[:, 1, :], m12[:, 0, 1, :], m12[:, 1, 0, :], ALU.add, [m12], [e12])
        tt(eng, H[:], e12[:], S_ap, ALU.add, [e12, S_tok], [H])

    iP128 = KS.index(128)
    NS = 3

    def pre_A(t):
        s = t % NS
        dma(POOL, xb[s][:], xprev[128 * t:128 * (t + 1), :], [], [xb[s]], "xb%d" % s)
        rms_stats(xb[s][:], junk[:], ssq[s], rstd[s], [xb[s]], junk)
        for hb in range(2):
            pst = banks[hb][:].bitcast(BF16)
            for j in range(8):
                k = hb * 8 + j
                tr(pst[:, 128 * j:128 * (j + 1)], xb[s][:, 128 * k:128 * (k + 1)], ident[:], [xb[s], ident], [bkb[hb]])
            cp(ACT, xT[s][:, 8 * hb:8 * hb + 8, :].rearrange("p k t -> p (k t)"), pst, [bkb[hb]], [xT[s]])

    def pre_B(t):
        s = t % NS
        ub = usb[(t // 2) % 2]
        for nh in range(2):
            for k in range(16):
                mm(banks[2 + nh][:], xT[s][:, k, :], Wu[:, k, 512 * nh:512 * (nh + 1)], k == 0, k == 15, [xT[s], Wu], [bkb[2 + nh]])
            act(ub[:, t % 2, 512 * nh:512 * (nh + 1)], banks[2 + nh][:], AF.Copy, [bkb[2 + nh], rstd[s]], [ub], scale=rstd[s][:])

    def pre_C(tl):
        nt = len(tl)
        ub = usb[(tl[0] // 2) % 2]
        for q in range(4):
            for hb_ in range(2):
                bk = 4 + 2 * (q % 2) + hb_
                vb = banks[bk][:].rearrange("p (a r c) -> p a r c", a=4, r=2)
                for a in range(4):
                    pair = 8 * q + 4 * hb_ + a
                    for ri in range(2):
                        mm(vb[:, a, ri, 0:32 * nt], ApowT[:, pair, ri, :], ub[:, 0:nt, 32 * pair:32 * (pair + 1)], True, True, [ApowT, ub], [bkb[bk]])
                pa = slice(8 * q + 4 * hb_, 8 * q + 4 * hb_ + 4)
                for half in range(2):
                    ps = slice(64 * half, 64 * half + 64)
                    vsel = vb[ps].rearrange("p a r (t c) -> p (a r) t c", t=2)[:, :, 0:nt, 16 * half:16 * half + 16]
                    for x_, Tx in enumerate((T1, T2)):
                        tb_ = Tx[ps, pa].rearrange("p a r c -> p (a r) c")[:, :, None, :].broadcast_to([64, 8, nt, 16])
                        tt(DVE, prod[ps, x_, pa].rearrange("p a r t c -> p (a r) t c")[:, :, 0:nt, :], vsel, tb_, ALU.mult, [bkb[bk], Tx], [prod])
        for ti in range(nt):
            P.op(DVE, lambda e, ti=ti: e.tensor_reduce(out=Stile[ti][:].rearrange("p x a -> p (x a)"), in_=prod[:].rearrange("p x a r t c -> p (x a) r t c")[:, :, :, ti, :], axis=AX.XY, op=ALU.add), [prod.b], [Stile[ti].b])
            cstep(DVE, Hst, iP128, Stile[ti][:], Stile[ti], m12, e12)

    pend = []
    for it in range(npre + 2):
        if it < npre:
            pre_A(it)
        if 0 <= it - 1 < npre:
            pre_B(it - 1)
            pend.append(it - 1)
        if len(pend) >= 3 or (it == npre + 1 and pend):
            tl = pend[:2] if (pend[0] % 2 == 0 and len(pend) >= 2) else pend[:1]
            pre_C(tl)
            pend = pend[len(tl):]
    while pend:
        tl = pend[:2] if (pend[0] % 2 == 0 and len(pend) >= 2) else pend[:1]
        pre_C(tl)
        pend = pend[len(tl):]
    if "hin" in dbg:
        dma(SP, dbg_out["hin"], Hst[:], [Hst], [], "dbg"); final_toks.append(Hst.b)
    free_scope(sB, [T1, T2, junk, m12, e12, prod] + Stile + xb + ssq + rstd + xT + usb)

    memset(DVE, uTd[:, :, 0:8, 64:80], 0.0, [uTd])
    sC1 = contextlib.ExitStack()
    xbC = [T(sC1, "xbC%d" % i, [128, 2048], BF16) for i in range(2)]
    xnC = [T(sC1, "xnC%d" % i, [128, 2048], BF16) for i in range(2)]
    junkC = T(sC1, "junkC", [128, 2048], BF16)
    ssC = [T(sC1, "ssC%d" % i, [128, 1]) for i in range(2)]; rsC = [T(sC1, "rsC%d" % i, [128, 1]) for i in range(2)]
    hT4 = T(sC1, "hT4", [128, 16, 4, 128], BF16)
    cnt = 0
    for batch in ((0, 1, 2, 3), (4, 5, 6, 7), (8,)):
        nb = len(batch)
        for ti, tile in enumerate(batch):
            s = cnt % 2; cnt += 1
            dma(POOL, xbC[s][:], xown[128 * tile:128 * (tile + 1), :], [], [xbC[s]], "xbC%d" % s)
            rms_stats(xbC[s][:], junkC[:], ssC[s], rsC[s], [xbC[s]], junkC)
            ts(DVE, xnC[s][:], xbC[s][:], rsC[s][:], None, ALU.mult, None, [xbC[s], rsC[s]], [xnC[s]])
            for hb in range(2):
                pst = banks[hb][:].bitcast(BF16)
                for j in range(8):
                    k = hb * 8 + j
                    tr(pst[:, 128 * j:128 * (j + 1)], xnC[s][:, 128 * k:128 * (k + 1)], ident[:], [xnC[s], ident], [bkb[hb]])
                cp(ACT, hT4[:, 8 * hb:8 * hb + 8, ti, :], pst.rearrange("p (k t) -> p k t", t=128), [bkb[hb]], [hT4])
        for ct in range(8):
            bk = 2 + ct % 2
            for k in range(16):
                mm(banks[bk][:, 0:128 * nb], Wu[:, k, 128 * ct:128 * (ct + 1)], hT4[:, k, 0:nb, :], k == 0, k == 15, [Wu, hT4], [bkb[bk]])
            if nb == 4:
                j0 = 8 * batch[0]
                cp(DVE if ct % 2 else ACT, uTd[:, ct, :, j0:j0 + 32].rearrange("p t j -> p j t"), banks[bk][:].rearrange("p (j t) -> p j t", t=16), [bkb[bk]], [uTd])
            else:
                cp(DVE if ct % 2 else ACT, uTd[:, ct, 8:16, 64:80].rearrange("p t j -> p j t"), banks[bk][:, 0:128].rearrange("p (j t) -> p j t", t=8), [bkb[bk]], [uTd])
    free_scope(sC1, xbC + xnC + [junkC] + ssC + rsC + [hT4])
    free_scope(sBC, [Wu])
    sC = contextlib.ExitStack()
    S_H = T(sC, "S_H", [128, 32, 2, NJ])
    Hprev = T(sC, "Hprev", [128, 32, 2, NJ], BF16)
    Hv = T(sC, "Hv", [128, 32, 2, 16])
    sC0 = contextlib.ExitStack()
    h0n = T(sC0, "h0n", [16, 2, 2048]); h0H = T(sC0, "h0H", [128, 32, 2, 16]); hvt = T(sC0, "hvt", [128, 4, 32, 16])
    for q in range(2):
        dma(SP, h0n[:, 0, :], st_re[:, 2048 * q:2048 * (q + 1)], [], [h0n], "h0n")
        dma(SP, h0n[:, 1, :], st_im[:, 2048 * q:2048 * (q + 1)], [], [h0n], "h0n")
        for ri in range(2):
            pb = banks[ri][:].rearrange("p (a b) -> p a b", b=16)
            for a in range(16):
                tr(pb[:, a, :], h0n[:, ri, 128 * a:128 * (a + 1)], identf[0:16, 0:16], [h0n, identf], [bkb[ri]])
            cp(ACT, h0H[:, 16 * q:16 * q + 16, ri, :], pb[:, 0:16, :], [bkb[ri]], [h0H])
    im8 = KS.index(-8)
    Ar8 = AK[:, im8, 0, :, None].broadcast_to([128, 32, 16]); Ai8 = AK[:, im8, 1, :, None].broadcast_to([128, 32, 16])
    tt(DVE, hvt[:, 0], h0H[:, :, 0, :], Ar8, ALU.mult, [h0H, AK], [hvt]); tt(DVE, hvt[:, 1], h0H[:, :, 1, :], Ai8, ALU.mult, [h0H, AK], [hvt])
    tt(DVE, hvt[:, 2], h0H[:, :, 1, :], Ar8, ALU.mult, [h0H, AK], [hvt]); tt(DVE, hvt[:, 3], h0H[:, :, 0, :], Ai8, ALU.mult, [h0H, AK], [hvt])
    tt(DVE, Hv[:, :, 0, :], hvt[:, 0], hvt[:, 1], ALU.subtract, [hvt], [Hv]); tt(DVE, Hv[:, :, 1, :], hvt[:, 2], hvt[:, 3], ALU.add, [hvt], [Hv])
    cp(DVE, Hprev[:, :, :, 64:80], Hv[:], [Hv], [Hprev])
    free_scope(sC0, [h0n, h0H, hvt])


    sC2 = contextlib.ExitStack()
    MBp = T(sC2, "MBp", [128, 4, 17, 2, 32], BF16)
    CAp = T(sC2, "CAp", [128, 4, 17, 2, 32], BF16)
    WSp = T(sC2, "WSp", [128, 16, 2, 128], BF16)
    tq = T(sC2, "tq", [128, 4, 4, 17, 16])
    Kd = T(sC2, "Kd", [128, 16, 32], BF16); Kfull = T(sC2, "Kfull", [128, 16, 4, 32], BF16)
    yv = T(sC2, "yv", [128, 16, NJ]); y2 = T(sC2, "y2", [128, 16, NJ])
    Hcur = T(sC2, "Hcur", [128, 2, 32]); Hcur2 = T(sC2, "Hcur2", [128, 2, 32])
    m12c = T(sC2, "m12c", [128, 2, 2, 32]); e12c = T(sC2, "e12c", [128, 2, 32])
    memset(DVE, MBp[:], 0.0, [MBp]); memset(DVE, CAp[:], 0.0, [CAp])
    ca_tok = Buf("ca_scr")

    def bd_table(eng, dst, X, ct, neg_im):
        pa = slice(4 * ct, 4 * ct + 4)
        Ar = AK[:, 0:17, 0, pa].rearrange("p k a -> p a k")[:, :, :, None].broadcast_to([128, 4, 17, 16])
        Ai = AK[:, 0:17, 1, pa].rearrange("p k a -> p a k")[:, :, :, None].broadcast_to([128, 4, 17, 16])
        Xr = X[:, 0, pa, None, :].broadcast_to([128, 4, 17, 16]); Xi = X[:, 1, pa, None, :].broadcast_to([128, 4, 17, 16])
        tt(eng, tq[:, 0], Ar, Xr, ALU.mult, [AK, X], [tq]); tt(eng, tq[:, 1], Ai, Xi, ALU.mult, [AK, X], [tq])
        tt(eng, tq[:, 2], Ai, Xr, ALU.mult, [AK, X], [tq]); tt(eng, tq[:, 3], Ar, Xi, ALU.mult, [AK, X], [tq])
        for half in range(2):
            ps = slice(64 * half, 64 * half + 64); cs = slice(16 * half, 16 * half + 16)
            tt(eng, dst[ps, :, :, 0, cs], tq[ps, 0], tq[ps, 1], ALU.subtract, [tq], [dst])
            if neg_im:
                stt(eng, dst[ps, :, :, 1, cs], tq[ps, 2], -1.0, tq[ps, 3], ALU.mult, ALU.subtract, [tq], [dst])
            else:
                tt(eng, dst[ps, :, :, 1, cs], tq[ps, 2], tq[ps, 3], ALU.add, [tq], [dst])

    bd_table(DVE, MBp, BbH, 0, False)
    for ct in range(8):
        for rnd in range(4):
            for kk in range(4):
                k = 4 * rnd + kk
                for ri in range(2):
                    bk = (kk * 2 + ri) // 4; col = ((kk * 2 + ri) % 4) * 128
                    for pl in range(4):
                        mm(banks[bk][32 * pl:32 * pl + 32, col:col + 128], MBp[:, pl, k, ri, :], ident[:], True, True, [MBp, ident], [bkb[bk]], tp=(0, 32 * pl))
            for bk in range(2):
                cp(ACT, WSp[:, 4 * rnd + 2 * bk:4 * rnd + 2 * bk + 2, :, :].rearrange("p k r q -> p (k r q)"), banks[bk][:], [bkb[bk]], [WSp])
        bd_table(DVE, CAp, cH, ct, True)
        dma(SP, ca_scr[ct], CAp[:].rearrange("p a k r c -> p (a k r c)"), [CAp], [ca_tok], "ca_out")
        if ct < 7:
            bd_table(DVE, MBp, BbH, ct + 1, False)
        for pl in range(4):
            bk = 2 + pl // 2
            sps = banks[bk][:, 0:2 * 2 * NJ].rearrange("p (a r j) -> p a r j", a=2, r=2)
            for two in range(2):
                for ri in range(2):
                    for s_ in range(16):
                        mm(sps[64 * two:64 * two + 64, pl % 2, ri, :], WSp[32 * pl:32 * pl + 32, 15 - s_, ri, 64 * two:64 * two + 64],
                           uTd[32 * pl:32 * pl + 32, ct, s_, :], s_ == 0, s_ == 15, [WSp, uTd], [bkb[bk]], tp=(32 * pl, 64 * two))
            if pl % 2 == 1:
                cp(ACT, S_H[:, 4 * ct + pl - 1:4 * ct + pl + 1, :, :], sps, [bkb[bk]], [S_H])

    i16 = KS.index(16)
    Hc = [Hcur, Hcur2]
    cp(DVE, Hc[0][:], Hst[:], [Hst], [Hc[0]])
    P16 = AK[:, i16]
    for j in range(64):
        a_, b_ = Hc[j % 2], Hc[(j + 1) % 2]
        cp(ACT, Hprev[:, :, :, j].rearrange("p a r -> p r a"), a_[:], [a_], [Hprev])
        tt(DVE, m12c[:], a_[:, None, :, :].broadcast_to([128, 2, 2, 32]), P16[:, :, None, :].broadcast_to([128, 2, 2, 32]), ALU.mult, [a_, AK], [m12c])
        tt(DVE, e12c[:, 0, :], m12c[:, 0, 0, :], m12c[:, 1, 1, :], ALU.subtract, [m12c], [e12c])
        tt(DVE, e12c[:, 1, :], m12c[:, 0, 1, :], m12c[:, 1, 0, :], ALU.add, [m12c], [e12c])
        tt(DVE, b_[:], e12c[:], S_H[:, :, :, j].rearrange("p a r -> p r a"), ALU.add, [e12c, S_H], [b_])
    dma(SP, ssm_fin, Hcur[:], [Hcur], [], "fin_out"); final_toks.append(Hcur.b)
    sC3 = contextlib.ExitStack()
    hft = T(sC3, "hft", [128, 4, 32, 16]); HfS = T(sC3, "HfS", [128, 32, 2, 16]); hfo = [T(sC3, "hfo%d" % i, [16, 512]) for i in range(2)]
    Ar16 = AK[:, i16, 0, :, None].broadcast_to([128, 32, 16]); Ai16 = AK[:, i16, 1, :, None].broadcast_to([128, 32, 16])
    tt(DVE, hft[:, 0], Hv[:, :, 0, :], Ar16, ALU.mult, [Hv, AK], [hft]); tt(DVE, hft[:, 1], Hv[:, :, 1, :], Ai16, ALU.mult, [Hv, AK], [hft])
    tt(DVE, hft[:, 2], Hv[:, :, 1, :], Ar16, ALU.mult, [Hv, AK], [hft]); tt(DVE, hft[:, 3], Hv[:, :, 0, :], Ai16, ALU.mult, [Hv, AK], [hft])
    tt(DVE, hft[:, 0], hft[:, 0], hft[:, 1], ALU.subtract, [hft], [hft]); tt(DVE, hft[:, 2], hft[:, 2], hft[:, 3], ALU.add, [hft], [hft])
    tt(DVE, HfS[:, :, 0, :], hft[:, 0], S_H[:, :, 0, 64:80], ALU.add, [hft, S_H], [HfS])
    tt(DVE, HfS[:, :, 1, :], hft[:, 2], S_H[:, :, 1, 64:80], ALU.add, [hft, S_H], [HfS])
    nst = 0
    for q in range(2):
        for ri in range(2):
            for a4 in range(4):
                bk = 4 + a4
                for a in range(4):
                    pair = 16 * q + 4 * a4 + a
                    tr(banks[bk][0:16, 128 * a:128 * (a + 1)], HfS[:, pair, ri, :], identf[:], [HfS, identf], [bkb[bk]])
                st = hfo[nst % 2]; nst += 1
                cp(ACT, st[:], banks[bk][0:16, :], [bkb[bk]], [st])
                c0_ = 2048 * q + 512 * a4
                dma(SP, ssm_samp[:, ri, c0_:c0_ + 512], st[:], [st], [], "hfo_out%d" % (nst % 2))
    final_toks.extend([hfo[0].b, hfo[1].b])
    CAbuf = [CAp, MBp]
    dma(SP, CAbuf[0][:].rearrange("p a k r c -> p (a k r c)"), ca_scr[0], [ca_tok], [CAbuf[0]], "ca_in0")
    for ct in range(8):
        CAp = CAbuf[ct % 2]
        for pl in range(4):
            for ri in range(2):
                mm(banks[0][32 * pl:32 * pl + 32, :], BbBD[:, 4 * ct + pl, ri, :], CAp[:, pl, 0:16, ri, :], ri == 0, ri == 1, [BbBD, CAp], [bkb[0]], tp=(0, 32 * pl))
        if ct < 7:
            nb_ = CAbuf[(ct + 1) % 2]
            dma(SP, nb_[:].rearrange("p a k r c -> p (a k r c)"), ca_scr[ct + 1], [ca_tok], [nb_], "ca_in%d" % ((ct + 1) % 2))
        cp(ACT, Kd[:].rearrange("p t c -> p (t c)"), banks[0][:], [bkb[0]], [Kd])
        tt(DVE, Kfull[:], Kd[:, :, None, :].broadcast_to([128, 16, 4, 32]), maskc[:, None, :, None].broadcast_to([128, 16, 4, 32]), ALU.mult, [Kd, maskc], [Kfull])
        ybk = [(1, 0, 6), (2, 6, 12), (3, 12, 16)]
        for bk, t0, t1 in ybk:
            yps = banks[bk][:, 0:(t1 - t0) * NJ].rearrange("p (t j) -> p t j", j=NJ)
            for tpr in range(t0, t1):
                for tau in range(tpr + 1):
                    mm(yps[:, tpr - t0, :], Kfull[:, tau].rearrange("p a c -> p (a c)"), uTd[:, ct, tpr - tau, :], tau == 0, False, [Kfull, uTd], [bkb[bk]])
                for pl in range(4):
                    for ri in range(2):
                        mm(yps[32 * pl:32 * pl + 32, tpr - t0, :], CAp[:, pl, tpr + 1, ri, :], Hprev[:, 4 * ct + pl, ri, :], False, ri == 1,
                           [CAp, Hprev], [bkb[bk]], tp=(0, 32 * pl))
            stt(DVE, yv[:, t0:t1, :], uTd[:, ct, t0:t1, :], Dcol[:, ct:ct + 1], yps, ALU.mult, ALU.add, [uTd, Dcol, bkb[bk]], [yv])
        act(y2[:], yv[:], AF.Square, [yv], [y2])
        act(y2[:], y2[:], AF.Identity, [y2, onesc], [y2], bias=onesc[:], scale=0.044715)
        tt(DVE, y2[:], y2[:], yv[:], ALU.mult, [y2, yv], [y2])
        act(y2[:], y2[:], AF.Sigmoid, [y2], [y2], scale=1.5957691216057308)
        tt(DVE, zT[:, ct, 0:1024].rearrange("p (j t) -> p t j", t=16), yv[:, :, 0:64], y2[:, :, 0:64], ALU.mult, [yv, y2], [zT])
        tt(DVE, zT[:, ct, 1024:1152].rearrange("p (b i) -> p i b", i=8), yv[:, 8:16, 64:80], y2[:, 8:16, 64:80], ALU.mult, [yv, y2], [zT])
    if "zT" in dbg:
        dma(SP, dbg_out["zT"], zT[:], [zT], [], "dbg"); final_toks.append(zT.b)
    if "uTd" in dbg:
        dma(SP, dbg_out["uTd"], uTd[:], [uTd], [], "dbg"); final_toks.append(uTd.b)
    if "S_H" in dbg:
        dma(SP, dbg_out["S_H"], S_H[:], [S_H], [], "dbg"); final_toks.append(S_H.b)
    free_scope(sC3, [hft, HfS] + hfo)
    free_scope(sC2, CAbuf + [WSp, tq, Kd, Kfull, yv, y2, Hcur, Hcur2, m12c, e12c])
    free_scope(sC, [S_H, Hprev, Hv])
    free_scope(sU, [uTd])
    free_scope(sA, [aH, ldtH, dtH, lamH, AK, bH, cH, BbH, fH, ApowT, Hst, BbBD])

    if dbg.get("stop_after_C"):
        P.emit(final_bufs=final_toks)
        es.close()
        return nc, P

    class V:
        def __init__(self, ap, name):
            self.ap = ap
            self.b = Buf(name, hazards)

        def __getitem__(self, k):
            return self.ap[k]

    def alias(new, old):
        if P_plan[0]:
            return
        ev = []
        for o in old:
            ev.extend(o.b.events())
        for n in new:
            for e_ in ev:
                n.b.add_read(e_)

    sD = contextlib.ExitStack()
    slots = [T(sD, "slot%d" % i, [128, 4096], BF16) for i in range(NSLOT)]
    arX = T(sD, "arX", [128, 24576], BF16); arY = T(sD, "arY", [128, 6144], BF16); arZ = T(sD, "arZ", [128, 9216])
    kAB = T(sD, "kAB", [128, 2, 2, 4, 128], BF16); Vtok = T(sD, "Vtok", [128, 4, 256], BF16)
    sT = T(sD, "sT", [128, 8, 384], BF16)
    xbD = [T(sD, "xbD%d" % i, [128, 2048], BF16) for i in range(2)]
    xnD = [T(sD, "xnD%d" % i, [128, 2048], BF16) for i in range(2)]
    ssD = [T(sD, "ssD%d" % i, [128, 1]) for i in range(2)]; rsD = [T(sD, "rsD%d" % i, [128, 1]) for i in range(2)]
    PT = [T(sD, "PT%d" % i, [128, 512], BF16) for i in range(2)]; PTp = T(sD, "PTp", [128, 512], BF16)
    PTm = PT
    rden = T(sD, "rden", [64, 512]); kvo = T(sD, "kvo", [128, 512])
    tmpb = [T(sD, "tmpb%d" % i, [128, 384], BF16) for i in range(2)]
    tmpf = [T(sD, "tmpf%d" % i, [128, 384]) for i in range(2)]
    hT = V(arX[:, 0:6144].rearrange("p (k t) -> p k t", t=384), "hT")
    sgT = V(arX[:, 6144:18432].rearrange("p (c t) -> p c t", t=384), "sgT")
    aT = V(arX[:, 18432:24576].rearrange("p (h t) -> p h t", t=384), "aT")
    actT = V(arX[:, :].rearrange("p (f t) -> p f t", t=384), "actT")
    qT = V(arY[:, 0:3072].rearrange("p (c t) -> p c t", t=384), "qT")
    hmT = V(arY[:, :].rearrange("p (k t) -> p k t", t=384), "hmT")
    mergedT = V(arZ[:, 0:3072].bitcast(BF16).rearrange("p (k t) -> p k t", t=384), "mergedT")
    x1 = V(arZ[:, 3072:9216].rearrange("p (a d) -> p a d", d=2048), "x1")
    zb = arZ[:, :].bitcast(BF16)
    ckb = V(zb[:, 0:4096].rearrange("p (b c) -> p b c", c=256), "ckb")
    cvb = V(zb[:, 4096:8192].rearrange("p (b c) -> p b c", c=256), "cvb")
    kcT = V(zb[:, 8192:12288].rearrange("p (b c w) -> p b c w", b=16, c=2), "kcT")
    PTc = V(zb[:, 12288:14336], "PTc")
    qB = V(zb[:, 14336:15360].rearrange("p (c t) -> p c t", t=128), "qB")

    P_plan = [True]
    cc_tok = Buf("cachecopy")
    plan = []
    wstate = {"next": 0, "issued": 0}

    def wget(dram_ap, nparts, shape):
        nel = int(np.prod(shape))
        if P_plan[0]:
            plan.append((dram_ap, nparts, shape, nel))
            return slots[0], slots[0][0:nparts, 0:nel].rearrange(_fmt(shape), **_kw(shape))
        i = wstate["next"]
        wstate["next"] += 1
        while wstate["issued"] < min(len(plan), i + LOOKAHEAD + 1):
            n = wstate["issued"]
            ap_, np_, shp_, nel_ = plan[n]
            sl = slots[n % NSLOT]
            dma(POOL, sl[0:np_, 0:nel_].rearrange(_fmt(shp_), **_kw(shp_)), ap_, [], [sl], "slot%d" % (n % NSLOT))
            wstate["issued"] += 1
        sl = slots[i % NSLOT]
        return sl, sl[0:nparts, 0:nel].rearrange(_fmt(shape), **_kw(shape))

    def _fmt(shape):
        return "p (a b) -> p a b"

    def _kw(shape):
        return {"b": shape[1]}

    realop, realdma = P.op, P.dma

    class StopD(Exception):
        pass

    def stage(name, g):
        if dbg.get("stopD") == (name, g):
            raise StopD()

    def phaseD():
        rot = [0]

        def obank():
            b = 2 + rot[0] % 3
            rot[0] += 1
            return b

        def load_norm_T(src_ap, s, dstT, col0, gcol, fp32_src=None):
            if fp32_src is None:
                dma(POOL, xbD[s][:], src_ap, [], [xbD[s]], "xbD%d" % s)
                xin, xtok = xbD[s][:], xbD[s]
            else:
                xin, xtok = fp32_src, x1
            xn_ = xnD[s]
            rms_stats(xin, xn_[:], ssD[s], rsD[s], [xtok], xn_)
            ts(DVE, xn_[:], xin, rsD[s][:], None, ALU.mult, None, [xtok, rsD[s]], [xn_])
            for hb in range(2):
                pst = banks[hb][:].bitcast(BF16)
                for j in range(8):
                    k = hb * 8 + j
                    tr(pst[:, 128 * j:128 * (j + 1)], xn_[:, 128 * k:128 * (k + 1)], ident[:], [xn_, ident], [bkb[hb]])
                tt(DVE, dstT[:, 8 * hb:8 * hb + 8, col0:col0 + 128], pst.rearrange("p (k t) -> p k t", t=128),
                   gcol[:, 8 * hb:8 * hb + 8, None].broadcast_to([128, 8, 128]), ALU.mult, [bkb[hb], gcol], [dstT])

        def k_proj(blk, bv, cols, ncol, slot_list):
            for ch in range(2):
                bk = obank()
                for k in range(16):
                    mm(banks[bk][:, 0:ncol], bv[:, k, 128 * ch:128 * ch + 128], hT[:, k, cols], k == 0, k == 15, [blk, hT], [bkb[bk]])
                for i, sl in enumerate(slot_list):
                    cp(ACT, kAB[:, ch, 0, sl, :], banks[bk][:, 128 * i:128 * i + 128], [bkb[bk]], [kAB])
                bk = obank()
                for k in range(16):
                    mm(banks[bk][64:128, 0:ncol], bv[:, k, 128 * ch:128 * ch + 64], hT[:, k, cols], k == 0, k == 15, [blk, hT], [bkb[bk]], tp=(0, 64))
                for k in range(16):
                    mm(banks[bk][0:64, 0:ncol], bv[:, k, 128 * ch + 64:128 * ch + 128], hT[:, k, cols], k == 0, k == 15, [blk, hT], [bkb[bk]], tp=(0, 0))
                for i, sl in enumerate(slot_list):
                    cp(ACT, kAB[:, ch, 1, sl, :], banks[bk][:, 128 * i:128 * i + 128], [bkb[bk]], [kAB])

        def v_proj(blk, bv, col0, slot, fp32_out_cols=None):
            bk = obank()
            for k in range(16):
                mm(banks[bk][:, 0:256], hT[:, k, col0:col0 + 128], bv[:, k, :], k == 0, k == 15, [blk, hT], [bkb[bk]])
            if fp32_out_cols is not None:
                cp(DVE, kvo[:, fp32_out_cols:fp32_out_cols + 256], banks[bk][:, 0:256], [bkb[bk]], [kvo])
                cp(ACT, Vtok[:, slot, :], kvo[:, fp32_out_cols:fp32_out_cols + 256], [kvo], [Vtok])
            elif slot is not None:
                cp(ACT, Vtok[:, slot, :], banks[bk][:, 0:256], [bkb[bk]], [Vtok])

        def attn_norm(kv, col0):
            tt(DVE, rden[:].rearrange("p (h t) -> p h t", t=128), banks[7][0:64, :].rearrange("p (h t) -> p h t", t=128),
               esink[0:64, 4 * kv:4 * kv + 4, None].broadcast_to([64, 4, 128]), ALU.add, [bkb[7], esink], [rden])
            P.op(DVE, lambda e: e.reciprocal(out=rden[:], in_=rden[:]), [rden.b], [rden.b])
            tt(DVE, aT[0:64, 4 * kv:4 * kv + 4, col0:col0 + 128].rearrange("p (c h) t -> p h c t", h=2),
               banks[6][0:64, :].rearrange("p (h c t) -> p h c t", h=2, c=2), rden[:].rearrange("p (h c t) -> p h c t", h=2, c=2),
               ALU.mult, [bkb[6], rden], [aT])

        def scores(kv, slot, col0, bankE, bankO, Etab, pti, perm=False):
            ch, par = kv // 2, kv % 2
            kE = kAB[0:64, ch, par, slot, :]
            kO = kAB[64:128, ch, 1 - par, slot, :]
            mm(banks[bankE][:, 0:256], kE, qT[0:64, 2 * kv:2 * kv + 2, col0:col0 + 128], True, True, [kAB, qT], [bkb[bankE]], tp=(0, 0))
            mm(banks[bankO][:, 0:256], kO, qT[64:128, 2 * kv:2 * kv + 2, col0:col0 + 128], True, True, [kAB, qT], [bkb[bankO]], tp=(64, 0))
            act(PT[pti][:, 0:256], banks[bankE][:, 0:256], AF.Exp, [bkb[bankE]], [PT[pti]])
            act(PT[pti][:, 256:512], banks[bankO][:, 0:256], AF.Exp, [bkb[bankO]], [PT[pti]])
            if not perm:
                tt(DVE, PTm[pti][:], PT[pti][:], Etab[:, 4 * kv:4 * kv + 4, :].rearrange("p h t -> p (h t)"), ALU.mult, [PT[pti], Etab], [PTm[pti]])
            else:
                tt(DVE, PTp[:].rearrange("p (b s i) -> p s b i", b=16, s=4), PT[pti][:].rearrange("p (s b i) -> p s b i", s=4, b=16),
                   Etab[:, 4 * kv:4 * kv + 4, :].rearrange("p s (b i) -> p s b i", i=8), ALU.mult, [PT[pti], Etab], [PTp])

        def attn_norm_sample(kv, col0):
            for h in range(2):
                dv = banks[7][0:64, :].rearrange("p (b h c i) -> p b h c i", b=16, h=2, c=2)[:, :, h]
                ov = banks[6][0:64, :].rearrange("p (b h c i) -> p b h c i", b=16, h=2, c=2)[:, :, h]
                rv = rden[:].rearrange("p (b h c i) -> p b h c i", b=16, h=2, c=2)[:, :, h]
                tt(DVE, rv, dv, esink[0:64, None, 4 * kv + 2 * h:4 * kv + 2 * h + 2, None].broadcast_to([64, 16, 2, 8]), ALU.add, [bkb[7], esink], [rden])
                P.op(DVE, lambda e, rv=rv: e.reciprocal(out=rv, in_=rv), [rden.b], [rden.b])
                tt(DVE, aT[0:64, 4 * kv:4 * kv + 4, col0:col0 + 128].rearrange("p (c h) (b i) -> p h b c i", h=2, i=8)[:, h], ov, rv, ALU.mult, [bkb[6], rden], [aT])

        for g, tiles in enumerate(GROUPS):
            if g not in dbg.get("groups", (0, 1, 2)):
                continue
            tok0 = 384 * g
            alias([hT, sgT, aT], [actT]); alias([qT], [hmT])
            if g == 2:
                alias([ckb, cvb, kcT, PTc, qB], [mergedT, x1])
            if g == 0:
                load_norm_T(xprev[128 * (npre - 1):128 * npre, :], 0, hT, 0, gcolA)
                blk, bv = wget(w_in[:, 1024:1280].rearrange("(k p) c -> p k c", p=128), 128, (16, 256))
                k_proj(blk, bv, slice(0, 128), 128, [0])
                blk, bv = wget(w_in[:, 1280:1536].rearrange("(k p) c -> p k c", p=128), 128, (16, 256))
                v_proj(blk, bv, 0, 0)
            for tl, tile in enumerate(tiles):
                load_norm_T(xown[128 * tile:128 * (tile + 1), :], tl % 2, hT, 128 * tl, gcolA)
            stage("D1a", g)
            for qb_ in range(4):
                blk, bv = wget(w_in[:, 256 * qb_:256 * qb_ + 256].rearrange("(k p) c -> p k c", p=128), 128, (16, 256))
                for half in range(2):
                    chunk = 2 * qb_ + half
                    bk = obank()
                    for k in range(16):
                        mm(banks[bk][:, 0:384], bv[:, k, 128 * half:128 * half + 128], hT[:, k, :], k == 0, k == 15, [blk, hT], [bkb[bk]])
                    act(qT[:, chunk, :], banks[bk][:, 0:384], AF.Copy, [bkb[bk]], [qT], scale=0.125)
                    if g == 2:
                        bk = obank()
                        for k in range(16):
                            mm(banks[bk][64:128, 0:128], bv[:, k, 128 * half:128 * half + 64], hT[:, k, 256:384], k == 0, k == 15, [blk, hT], [bkb[bk]], tp=(0, 64))
                        for k in range(16):
                            mm(banks[bk][0:64, 0:128], bv[:, k, 128 * half + 64:128 * half + 128], hT[:, k, 256:384], k == 0, k == 15, [blk, hT], [bkb[bk]], tp=(0, 0))
                        act(qB[:, chunk, :], banks[bk][:, 0:128], AF.Copy, [bkb[bk]], [qB], scale=0.125)
            stage("D1b", g)
            blk, bv = wget(w_in[:, 1024:1280].rearrange("(k p) c -> p k c", p=128), 128, (16, 256))
            k_proj(blk, bv, slice(0, 384), 384, [(t + 1) % 4 for t in tiles])
            stage("D1c", g)
            for tl, tile in enumerate(tiles):
                if tile >= 7:
                    bk = obank()
                    for k in range(16):
                        mm(banks[bk][:, 0:256], hT[:, k, 128 * tl:128 * tl + 128], bv[:, k, :], k == 0, k == 15, [blk, hT], [bkb[bk]])
                    cp(DVE, kvo[:, 0:256], banks[bk][:, 0:256], [bkb[bk]], [kvo])
                    if tile == 7:
                        dma(SP, kv_last[0], kvo[:, 0:256], [kvo], [], "kvo_out")
            blk, bv = wget(w_in[:, 1280:1536].rearrange("(k p) c -> p k c", p=128), 128, (16, 256))
            for tl, tile in enumerate(tiles):
                v_proj(blk, bv, 128 * tl, (tile + 1) % 4, 256 if tile >= 7 else None)
                if tile == 7:
                    dma(SP, kv_last[1], kvo[:, 256:512], [kvo], [], "kvo_out")
                if tile == 8 and not dbg.get("no_ksamp"):
                    dma(SP, ksamp[:, 120:128, :], kvo[:, 0:256], [kvo], [], "kvo_out")
                    dma(SP, vsamp[:, 120:128, :], kvo[:, 256:512], [kvo], [], "kvo_out")
            stage("D1d", g)
            for gb in range(16):
                blk, bv = wget(w_in[:, 2560 + 256 * gb:2560 + 256 * gb + 256].rearrange("(k p) c -> p k c", p=128), 128, (16, 256))
                for half in range(2):
                    chunk = 2 * gb + half
                    bk = obank()
                    for k in range(16):
                        mm(banks[bk][:, 0:384], bv[:, k, 128 * half:128 * half + 128], hT[:, k, :], k == 0, k == 15, [blk, hT], [bkb[bk]])
                    act(sgT[:, chunk, :], banks[bk][:, 0:384], AF.Sigmoid, [bkb[bk]], [sgT])
            stage("D1", g)
            for tl, tile in enumerate(tiles):
                col0 = 128 * tl
                if tile < 8:
                    for kv in range(4):
                        scores(kv, tile % 4, col0, 2, 3, Eprev, 0)
                        if tile == 0:
                            ts(DVE, PTm[0][:], PTm[0][:], padm[:], None, ALU.mult, None, [PTm[0], padm], [PTm[0]])
                        stage("D2a", g)
                        scores(kv, (tile + 1) % 4, col0, 4, 5, Ecur, 1)
                        stage("D2b", g)
                        for i, sl in enumerate((tile % 4, (tile + 1) % 4)):
                            mm(banks[6][0:64, :], Vtok[:, sl, 64 * kv:64 * kv + 64], PTm[i][:], i == 0, i == 1, [Vtok, PTm[i]], [bkb[6]])
                            mm(banks[7][0:64, :], ones_bf[:, 0:64], PTm[i][:], i == 0, i == 1, [ones_bf, PTm[i]], [bkb[7]])
                        stage("D2c", g)
                        attn_norm(kv, col0)
                        stage("D2d", g)
                else:
                    dma(POOL, ckb[:], cache_k.rearrange("b w c -> w b c"), [], [ckb], "ckb")
                    dma(POOL, cvb[:], cache_v.rearrange("b w c -> w b c"), [], [cvb], "cvb")
                    dma(SP, ksamp[:, 0:120, :], cache_k[:, 8:128, :], [cc_tok], [], "cache_copy")
                    dma(SP, vsamp[:, 0:120, :], cache_v[:, 8:128, :], [cc_tok], [], "cache_copy")
                    for b in range(16):
                        bk = b % 4
                        pst = banks[bk][:].bitcast(BF16)
                        if True:
                            for c2 in range(2):
                                tr(pst[:, 128 * c2:128 * c2 + 128], ckb[:, b, 128 * c2:128 * c2 + 128], ident[:], [ckb, ident], [bkb[bk]])
                            cp(ACT, kcT[:, b, :, :].rearrange("p c w -> p (c w)"), pst[:, 0:256], [bkb[bk]], [kcT])
                    for b in range(16):
                        for kv in range(4):
                            bk = (kv % 2) * 2 + b // 8
                            psc = banks[bk][:].rearrange("p (b k s i) -> p b k s i", b=8, k=2, s=4)
                            be = 64 * (kv % 2)
                            s_nat, s_swp = ((0, 2) if kv % 2 == 0 else (2, 0))
                            mm(psc[:, b % 8, kv // 2, s_nat:s_nat + 2, :].rearrange("p s i -> p (s i)"), kcT[be:be + 64, b, kv // 2, :], qT[be:be + 64, 2 * kv:2 * kv + 2, 256 + 8 * b:256 + 8 * b + 8],
                               True, True, [kcT, qT], [bkb[bk]], tp=(be, 0))
                            mm(psc[:, b % 8, kv // 2, s_swp:s_swp + 2, :].rearrange("p s i -> p (s i)"), kcT[be:be + 64, b, kv // 2, :], qB[be:be + 64, 2 * kv:2 * kv + 2, 8 * b:8 * b + 8],
                               True, True, [kcT, qB], [bkb[bk]], tp=(be, 0))
                    for bk in range(4):
                        act(PTc[:, 512 * bk:512 * bk + 512], banks[bk][:], AF.Exp, [bkb[bk]], [PTc])
                    for kvpar in range(2):
                        for bh in range(2):
                            for kvh in range(2):
                                kvv = 2 * kvh + kvpar
                                o_ = 1024 * kvpar + 512 * bh
                                v_ = PTc[:, o_:o_ + 512].rearrange("p (b k s i) -> p b k s i", b=8, k=2, s=4)[:, :, kvh]
                                tt(DVE, v_, v_, Ecache[:, None, 4 * kvv:4 * kvv + 4, :].broadcast_to([128, 8, 4, 8]), ALU.mult, [PTc, Ecache], [PTc])
                    for kv in range(4):
                        scores(kv, (tile + 1) % 4, col0, 4, 5, Enew, 1, perm=True)
                        mm(banks[6][0:64, :], Vtok[:, (tile + 1) % 4, 64 * kv:64 * kv + 64], PTp[:], True, False, [Vtok, PTp], [bkb[6]])
                        mm(banks[7][0:64, :], ones_bf[:, 0:64], PTp[:], True, False, [ones_bf, PTp], [bkb[7]])
                        for b in range(16):
                            o_ = 1024 * (kv % 2) + 512 * (b // 8)
                            pr = PTc[:, o_:o_ + 512].rearrange("p (b k x) -> p b k x", b=8, k=2)[:, b % 8, kv // 2, :]
                            mm(banks[6][0:64, 32 * b:32 * b + 32], cvb[:, b, 64 * kv:64 * kv + 64], pr, False, b == 15, [cvb, PTc], [bkb[6]])
                            mm(banks[7][0:64, 32 * b:32 * b + 32], ones_bf[:, 0:64], pr, False, b == 15, [ones_bf, PTc], [bkb[7]])
                        attn_norm_sample(kv, col0)
                    alias([mergedT, x1], [ckb, cvb, kcT, PTc, qB])
            stage("D2", g)
            for gb in range(4):
                blk, bv = wget(w_glu[:, 256 * gb:256 * gb + 256].rearrange("(k p) c -> p k c", p=128), 128, (8, 256))
                for half in range(2):
                    chunk = 2 * gb + half
                    bk = obank()
                    for k in range(8):
                        mm(banks[bk][:, 0:384], bv[:, k, 128 * half:128 * half + 128], zT[:, k, tok0:tok0 + 384], k == 0, k == 7, [blk, zT], [bkb[bk]])
                    tb_ = tmpb[chunk % 2]
                    act(tb_[:], banks[bk][:, 0:384], AF.Sigmoid, [bkb[bk], bglu], [tb_], bias=bglu[:, chunk:chunk + 1])
                    tt(DVE, sT[:, chunk, :], tb_[:], zT[:, chunk, tok0:tok0 + 384], ALU.mult, [tb_, zT], [sT])
            stage("D3", g)
            for cb in range(8):
                blkA, bvA = wget(w_ab[:, 256 * cb:256 * cb + 256].rearrange("(h p) c -> p h c", p=64), 64, (16, 256))
                blkS, bvS = wget(w_sb[:, 256 * cb:256 * cb + 256].rearrange("(k p) c -> p k c", p=128), 128, (8, 256))
                for half in range(2):
                    c = 2 * cb + half
                    bka = obank()
                    for h in range(16):
                        mm(banks[bka][:, 0:384], bvA[:, h, 128 * half:128 * half + 128], aT[0:64, h, :], h == 0, h == 15, [blkA, aT], [bkb[bka]])
                    bks = obank()
                    for k in range(8):
                        mm(banks[bks][:, 0:384], bvS[:, k, 128 * half:128 * half + 128], sT[:, k, :], k == 0, k == 7, [blkS, sT], [bkb[bks]])
                    tt(DVE, tmpf[0][:], banks[bka][:, 0:384], sgT[:, c, :], ALU.mult, [bkb[bka], sgT], [tmpf[0]])
                    tt(DVE, tmpf[1][:], banks[bks][:, 0:384], sgT[:, 16 + c, :], ALU.mult, [bkb[bks], sgT], [tmpf[1]])
                    tt(DVE, mergedT[:, c, :], tmpf[0][:], tmpf[1][:], ALU.add, [tmpf[0], tmpf[1]], [mergedT])
            stage("D4", g)
            for tl, tile in enumerate(tiles):
                dma(SP, x1[:, tl, :], xown[128 * tile:128 * (tile + 1), :], [], [x1], "x1in")
            for cb in range(4):
                blk0, bv0 = wget(w_out[0:1024, 512 * cb:512 * cb + 512].rearrange("(k p) c -> p k c", p=128), 128, (8, 512))
                blk1, bv1 = wget(w_out[1024:2048, 512 * cb:512 * cb + 512].rearrange("(k p) c -> p k c", p=128), 128, (8, 512))
                for tl in range(3):
                    bk = 5 + tl
                    for k in range(16):
                        bv_, blk_ = (bv0, blk0) if k < 8 else (bv1, blk1)
                        mm(banks[bk][:], mergedT[:, k, 128 * tl:128 * tl + 128], bv_[:, k % 8, :], k == 0, k == 15, [mergedT, blk_], [bkb[bk]])
                    tt(DVE, x1[:, tl, 512 * cb:512 * cb + 512], banks[bk][:], x1[:, tl, 512 * cb:512 * cb + 512], ALU.add, [bkb[bk], x1], [x1])
            stage("D5", g)
            alias([hmT], [qT])
            for tl in range(3):
                load_norm_T(None, tl % 2, hmT, 128 * tl, gcolM, fp32_src=x1[:, tl, :])
            stage("D6", g)
            alias([actT], [hT, sgT, aT])
            for ub in range(dbg.get("d7_ub", 32)):
                blk, bv = wget(w_up[:, 256 * ub:256 * ub + 256].rearrange("(k p) c -> p k c", p=128), 128, (16, 256))
                for half in range(2):
                    f = 2 * ub + half
                    bk = obank()
                    for k in range(16):
                        mm(banks[bk][:, 0:384], bv[:, k, 128 * half:128 * half + 128], hmT[:, k, :], k == 0, k == 15, [blk, hmT], [bkb[bk]])
                    tb_ = tmpb[f % 2]
                    act(tb_[:], banks[bk][:, 0:384], AF.Relu, [bkb[bk]], [tb_])
                    tt(DVE, actT[:, f, :], tb_[:], tb_[:], ALU.mult, [tb_], [actT])
            stage("D7", g)
            for cb in range(dbg.get("d8_cb", 4)):
                for rg in range(dbg.get("d8_rg", 8)):
                    blk, bv = wget(w_down[1024 * rg:1024 * rg + 1024, 512 * cb:512 * cb + 512].rearrange("(k p) c -> p k c", p=128), 128, (8, 512))
                    for tl in range(3):
                        for f8 in range(8):
                            mm(banks[5 + tl][:], actT[:, 8 * rg + f8, 128 * tl:128 * tl + 128], bv[:, f8, :], rg == 0 and f8 == 0, rg == dbg.get("d8_rg", 8) - 1 and f8 == 7,
                               [actT, blk], [bkb[5 + tl]])
                for tl in range(3):
                    tt(DVE, x1[:, tl, 512 * cb:512 * cb + 512], banks[5 + tl][:], x1[:, tl, 512 * cb:512 * cb + 512], ALU.add, [bkb[5 + tl], x1], [x1])
            stage("D8", g)
            for tl, tile in enumerate(tiles):
                s = tl % 2
                rms_stats(x1[:, tl, :], xnD[s][:], ssD[s], rsD[s], [x1], xnD[s])
                stt(DVE, x1[:, tl, :], x1[:, tl, :], rsD[s][:], gfin[:], ALU.mult, ALU.mult, [x1, rsD[s], gfin], [x1])
                dma(SP, y_own[128 * tile:128 * (tile + 1), :], x1[:, tl, :], [x1], [], "yout")
            stage("D9", g)

    P.op = lambda *a, **k: None
    P.dma = lambda *a, **k: None
    try:
        phaseD()
    except StopD:
        pass
    P.op, P.dma = realop, realdma
    P_plan[0] = False
    try:
        phaseD()
    except StopD:
        pass
    final_toks.extend([x1.b, kvo.b, cc_tok])
    print("sbuf bytes remaining", nc.sbuf_bytes_remaining)
    P.emit(final_bufs=final_toks)
    sD.close()
    es.close()
    return nc, P


def _prep_in_maps(I, npre=NPRE, ncores=NCORES):
    C = _host_consts()
    f32 = lambda a: np.ascontiguousarray(np.asarray(a, dtype=np.float32))
    xp = f32(I["x_prompt"])[0]
    full = np.concatenate([np.zeros((112, 2048), np.float32), f32(I["meta_tokens"]), xp], 0)
    xs = f32(I["x_sample"]).reshape(128 * 8, 2048)
    ck = f32(I["cache_k"]).reshape(128, 128, 256); cv = f32(I["cache_v"]).reshape(128, 128, 256)
    sr = f32(I["state_ssm_re"]).reshape(128, 4096); si = f32(I["state_ssm_im"]).reshape(128, 4096)
    shared = {k: f32(I[k]) for k in ("g_attn_norm", "g_mlp_norm", "g_final_norm", "w_in", "ssm_a_re", "ssm_a_im", "ssm_log_dt",
                                     "ssm_b_re", "ssm_b_im", "ssm_c_re", "ssm_c_im", "ssm_d", "w_glu", "b_glu", "w_attn_branch",
                                     "w_ssm_branch", "w_out", "w_up", "w_down")}
    shared["sinks_perm"] = f32(I["sinks"])[C["hperm"]]
    for k in ("ident_bf", "ident_f", "kcol", "maskc", "e_cur", "e_new", "e_cache"):
        shared[k] = C[k]
    maps = []
    for c in range(ncores):
        end = 128 + 1024 * c
        st = end - npre * 128
        seg = full[max(st, 0):end]
        if st < 0:
            seg = np.concatenate([np.zeros((-st, 2048), np.float32), seg], 0)
        m = dict(shared)
        m["xprev"] = np.ascontiguousarray(seg)
        m["xown"] = np.ascontiguousarray(np.concatenate([full[end:end + 1024], xs[128 * c:128 * (c + 1)]], 0))
        m["cache_k"] = np.ascontiguousarray(ck[16 * c:16 * (c + 1)]); m["cache_v"] = np.ascontiguousarray(cv[16 * c:16 * (c + 1)])
        m["st_re"] = np.ascontiguousarray(sr[16 * c:16 * (c + 1)]); m["st_im"] = np.ascontiguousarray(si[16 * c:16 * (c + 1)])
        m["e_prev"] = C["e_prev"]
        m["padmask"] = (np.arange(128) >= 112).astype(np.float32).reshape(128, 1) if c == 0 else np.ones((128, 1), np.float32)
        maps.append(m)
    return maps


_CACHE = {}


def kernel(**inputs):
    if "prog" not in _CACHE:
        _CACHE["prog"] = build_program()[0]
    nc = _CACHE["prog"]
    maps = _prep_in_maps(inputs)
    res = run_bass_kernel_spmd(nc, maps, core_ids=list(range(NCORES)))
    R = res.results
    y_prompt = np.concatenate([R[c]["y_own"][:1024] for c in range(NCORES)], 0).reshape(1, 8192, 2048)
    y_sample = np.concatenate([R[c]["y_own"][1024:] for c in range(NCORES)], 0).reshape(128, 8, 2048)
    k_prompt = R[7]["kv_last"][0].reshape(1, 128, 4, 64)
    v_prompt = R[7]["kv_last"][1].reshape(1, 128, 4, 64)

    def fromH(h):
        h = h.reshape(2, 64, 2, 32)
        return (np.ascontiguousarray(h[:, :, 0, :].transpose(2, 0, 1)).reshape(1, 64, 64),
                np.ascontiguousarray(h[:, :, 1, :].transpose(2, 0, 1)).reshape(1, 64, 64))
    re_p, im_p = fromH(R[7]["ssm_fin"])
    k_sample = np.concatenate([R[c]["ksamp"] for c in range(NCORES)], 0).reshape(128, 128, 4, 64)
    v_sample = np.concatenate([R[c]["vsamp"] for c in range(NCORES)], 0).reshape(128, 128, 4, 64)
    ss = np.concatenate([R[c]["ssm_samp"] for c in range(NCORES)], 0)
    re_s = np.ascontiguousarray(ss[:, 0]).reshape(128, 64, 64)
    im_s = np.ascontiguousarray(ss[:, 1]).reshape(128, 64, 64)
    f = lambda a: np.ascontiguousarray(a, dtype=np.float32)
    return (f(y_prompt), f(y_sample), f(k_prompt), f(v_prompt), f(re_p), f(im_p), f(k_sample), f(v_sample), f(re_s), f(im_s))
```

```python
import contextlib
import math
import numpy as np
import ml_dtypes
import concourse.bass as bass
import concourse.mybir as mybir
from concourse.bass_utils import run_bass_kernel_spmd

F32 = mybir.dt.float32
BF16 = mybir.dt.bfloat16
I32 = mybir.dt.int32
ALU = mybir.AluOpType
AF = mybir.ActivationFunctionType
AX = mybir.AxisListType
PE, ACT, DVE, POOL, SP = "tensor", "scalar", "vector", "gpsimd", "sync"
COMPUTE = (PE, ACT, DVE, POOL)
TWO_PI = 2.0 * math.pi

NCORES = 8
NPRE = 57
NOWN = 9
GROUPS = [(0, 1, 2), (3, 4, 5), (6, 7, 8)]
NSLOT = 4
LOOKAHEAD = 2


class Buf:
    __slots__ = ("name", "last_write", "reads", "excl")

    def __init__(self, name, hazards=(), excl=False):
        self.name = name
        self.excl = excl
        self.last_write = None
        self.reads = {}
        for ev in hazards:
            self.add_read(ev)

    def add_read(self, ev):
        k = (ev[0], ev[1])
        if k not in self.reads or self.reads[k][2] < ev[2]:
            self.reads[k] = ev

    def events(self):
        ev = list(self.reads.values())
        if self.last_write is not None:
            ev.append(self.last_write)
        return ev


class Op:
    __slots__ = ("eng", "fn", "deps", "is_dma", "dsem", "dval", "idx", "signal")


class Prog:
    def __init__(self, nc, same_engine_sync=True):
        self.nc = nc
        self.same_engine_sync = same_engine_sync
        self.raw_only = True
        self.streams = {e: [] for e in (PE, ACT, DVE, POOL, SP)}
        self.dma_sems = {}
        self.n_ops = 0
        self.n_waits = 0
        self.waits_by = {}

    def _collect(self, reads, writes, eng=None):
        deps = []
        for b in reads:
            if b.last_write is not None:
                deps.append(b.last_write)
            if b.excl:
                deps.extend(ev for k, ev in b.reads.items() if not (k[0] == "E" and k[1] == eng))
        for b in writes:
            if b.last_write is not None and not (self.raw_only and b.last_write[0] == "E" and b.last_write[1] == eng):
                deps.append(b.last_write)
            deps.extend(ev for k, ev in b.reads.items() if not (self.raw_only and k[0] == "E" and k[1] == eng))
        return deps

    def _commit(self, ev, reads, writes):
        for b in reads:
            b.add_read(ev)
        for b in writes:
            b.last_write = ev
            b.reads = {}

    def op(self, eng, fn, reads=(), writes=()):
        o = Op()
        o.eng, o.fn, o.is_dma, o.signal = eng, fn, False, False
        o.deps = self._collect(reads, writes, eng)
        st = self.streams[eng]
        o.idx = len(st)
        st.append(o)
        self._commit(("E", eng, o.idx), reads, writes)
        self.n_ops += 1
        return o

    def dma(self, queue, fn, reads=(), writes=(), sem=None):
        o = Op()
        o.eng, o.fn, o.is_dma, o.signal = queue, fn, True, False
        key = sem if sem is not None else (writes[0].name if writes else reads[0].name)
        o.deps = [d for d in self._collect(reads, writes) if not (d[0] == "D" and d[1] == key)]
        cnt = self.dma_sems.get(key, 0) + 16
        self.dma_sems[key] = cnt
        o.dsem, o.dval = key, cnt
        st = self.streams[queue]
        o.idx = len(st)
        st.append(o)
        self._commit(("D", key, cnt), reads, writes)
        self.n_ops += 1
        return o

    def emit(self, final_bufs=()):
        nc = self.nc
        fin = Op()
        fin.eng, fin.fn, fin.is_dma, fin.signal = SP, None, False, False
        fin.deps = []
        for b in final_bufs:
            fin.deps.extend(b.events())
        fin.idx = len(self.streams[SP])
        self.streams[SP].append(fin)
        for e, st in self.streams.items():
            for o in st:
                for d in o.deps:
                    if d[0] == "E" and not (d[1] == e and (e == PE or not self.same_engine_sync)):
                        self.streams[d[1]][d[2]].signal = True
        val = {}
        for e in COMPUTE:
            c, v = 0, []
            for o in self.streams[e]:
                if o.signal:
                    c += 1
                v.append(c)
            val[e] = v
        self.maxvals = {e: (val[e][-1] if val[e] else 0) for e in COMPUTE}
        stack = contextlib.ExitStack()
        esem = {e: stack.enter_context(nc.semaphore("es_" + e)) for e in COMPUTE}
        dsem = {k: stack.enter_context(nc.semaphore("ds_%d" % i)) for i, k in enumerate(self.dma_sems)}
        block = stack.enter_context(nc.Block())

        def make(e):
            st = self.streams[e]

            def body(eng):
                waited = {}
                for o in st:
                    need = {}
                    for d in o.deps:
                        if d[0] == "E":
                            if d[1] == e and (e == PE or not self.same_engine_sync):
                                continue
                            s, v = esem[d[1]], val[d[1]][d[2]]
                        else:
                            s, v = dsem[d[1]], d[2]
                        k = id(s)
                        if v > need.get(k, (None, 0))[1]:
                            need[k] = (s, v)
                    for k, (s, v) in need.items():
                        if v > waited.get(k, 0):
                            eng.wait_ge(s, v)
                            waited[k] = v
                            self.n_waits += 1
                            self.waits_by[e] = self.waits_by.get(e, 0) + 1
                    if o.fn is None:
                        continue
                    ins = o.fn(eng)
                    if o.is_dma:
                        ins.then_inc(dsem[o.dsem], 16)
                    elif o.signal:
                        ins.then_inc(esem[e], 1)
            return body

        for e in (PE, ACT, DVE, POOL, SP):
            if self.streams[e]:
                getattr(block, e)(make(e))
        stack.close()


def _host_consts():
    H = 16
    slopes = 2.0 ** (-8.0 * np.arange(1, H + 1, dtype=np.float64) / H)
    hperm = [4 * kv + i for kv in range(4) for i in (0, 2, 1, 3)]
    sl = slopes[hperm][None, :, None]
    k = np.arange(128)[:, None, None].astype(np.float64)
    q = np.arange(128)[None, None, :].astype(np.float64)
    e_cur = np.where(k <= q, np.exp(-sl * (q - k)), 0.0)
    e_prev = np.where(k >= q, np.exp(-sl * (q + 128 - k)), 0.0)
    e_prev0 = e_prev.copy()
    e_prev0[:112] = 0.0
    kb, ki = np.arange(128)[:, None, None] // 8, np.arange(128)[:, None, None] % 8
    qb, qi = np.arange(128)[None, None, :] // 8, np.arange(128)[None, None, :] % 8
    e_new = np.where((kb == qb) & (ki <= qi), np.exp(-sl * (qi - ki)), 0.0)
    w = np.arange(128)[:, None, None].astype(np.float64)
    i8 = np.arange(8)[None, None, :].astype(np.float64)
    e_cache = np.where(w >= i8, np.exp(-sl * (128 + i8 - w)), 0.0)
    bf = ml_dtypes.bfloat16
    maskc = (np.arange(128)[:, None] // 32 == np.arange(4)[None, :]).astype(np.float32)
    return dict(
        e_cur=e_cur.astype(np.float32).astype(bf), e_prev=e_prev.astype(np.float32).astype(bf),
        e_prev0=e_prev0.astype(np.float32).astype(bf), e_new=e_new.astype(np.float32).astype(bf),
        e_cache=e_cache.astype(np.float32).astype(bf),
        ident_bf=np.eye(128, dtype=np.float32).astype(bf), ident_f=np.eye(128, dtype=np.float32),
        kcol=(127 - np.arange(128, dtype=np.float32)).reshape(128, 1), maskc=maskc, hperm=hperm)


def build_program(npre=NPRE, dbg=None):
    dbg = dbg or {}
    nc = bass.Bass("TRN2", target_bir_lowering=False)
    es = contextlib.ExitStack()
    P = Prog(nc, same_engine_sync=not dbg.get("no_ses"))
    hazards = []

    def din(name, shape, dt=F32):
        return nc.dram_tensor(name, list(shape), dt, kind="ExternalInput").ap()

    def dout(name, shape, dt=F32):
        return nc.dram_tensor(name, list(shape), dt, kind="ExternalOutput").ap()

    class T:
        def __init__(self, stack, name, shape, dt=F32):
            self.t = stack.enter_context(nc.sbuf_tensor("s_" + name, list(shape), dt))
            self.b = Buf(name, hazards)

        def __getitem__(self, k):
            return self.t[k]

    def free_scope(stack, tensors):
        for t in tensors:
            hazards.extend(t.b.events())
        stack.close()

    def toks(xs):
        return [x.b if hasattr(x, "b") else x for x in xs]

    def tt(eng, out, in0, in1, op, R, W):
        P.op(eng, lambda e: e.tensor_tensor(out=out, in0=in0, in1=in1, op=op), toks(R), toks(W))

    def ts(eng, out, in0, s1, s2, op0, op1, R, W):
        if s2 is None:
            P.op(eng, lambda e: e.tensor_scalar(out=out, in0=in0, scalar1=s1, scalar2=None, op0=op0), toks(R), toks(W))
        else:
            P.op(eng, lambda e: e.tensor_scalar(out=out, in0=in0, scalar1=s1, scalar2=s2, op0=op0, op1=op1), toks(R), toks(W))

    def stt(eng, out, in0, scalar, in1, op0, op1, R, W):
        P.op(eng, lambda e: e.scalar_tensor_tensor(out=out, in0=in0, scalar=scalar, in1=in1, op0=op0, op1=op1), toks(R), toks(W))

    def cp(eng, out, in_, R, W):
        if eng == ACT:
            P.op(eng, lambda e: e.copy(out=out, in_=in_), toks(R), toks(W))
        else:
            P.op(eng, lambda e: e.tensor_copy(out=out, in_=in_), toks(R), toks(W))

    def act(out, in_, func, R, W, bias=None, scale=None, accum=None):
        kw = {}
        if bias is not None:
            kw["bias"] = bias
        if scale is not None:
            kw["scale"] = scale
        if accum is not None:
            kw["accum_out"] = accum
        P.op(ACT, lambda e: e.activation(out=out, in_=in_, func=func, **kw), toks(R), toks(W))

    def mm(out, lhsT, rhs, start, stop, R, W, tp=None):
        if tp is None:
            P.op(PE, lambda e: e.matmul(out, lhsT=lhsT, rhs=rhs, start=start, stop=stop), toks(R), toks(W))
        else:
            P.op(PE, lambda e: e.matmul(out, lhsT=lhsT, rhs=rhs, start=start, stop=stop, tile_position=tp), toks(R), toks(W))

    def tr(out, in_, idn, R, W):
        P.op(PE, lambda e: e.transpose(out=out, in_=in_, identity=idn), toks(R), toks(W))

    def dma(q, out, in_, R, W, sem, slow=False):
        if slow:
            P.dma(q, lambda e: e.dma_start(out=out, in_=in_, allow_slow_non_contiguous=True), toks(R), toks(W), sem=sem)
        else:
            P.dma(q, lambda e: e.dma_start(out=out, in_=in_), toks(R), toks(W), sem=sem)

    def memset(eng, out, v, W):
        P.op(eng, lambda e: e.memset(out, v), [], toks(W))

    xprev = din("xprev", [npre * 128, 2048])
    xown = din("xown", [NOWN * 128, 2048])
    cache_k = din("cache_k", [16, 128, 256]); cache_v = din("cache_v", [16, 128, 256])
    st_re = din("st_re", [16, 4096]); st_im = din("st_im", [16, 4096])
    g_attn = din("g_attn_norm", [2048]); g_mlp = din("g_mlp_norm", [2048]); g_fin = din("g_final_norm", [2048])
    w_in = din("w_in", [2048, 6656]); sinks_p = din("sinks_perm", [16])
    a_re = din("ssm_a_re", [64, 64]); a_im = din("ssm_a_im", [64, 64]); log_dt = din("ssm_log_dt", [64])
    b_re = din("ssm_b_re", [64, 64, 16]); b_im = din("ssm_b_im", [64, 64, 16])
    c_re = din("ssm_c_re", [64, 16, 64]); c_im = din("ssm_c_im", [64, 16, 64])
    ssm_d = din("ssm_d", [1024]); w_glu = din("w_glu", [1024, 1024]); b_glu = din("b_glu", [1024])
    w_ab = din("w_attn_branch", [1024, 2048]); w_sb = din("w_ssm_branch", [1024, 2048])
    w_out = din("w_out", [2048, 2048]); w_up = din("w_up", [2048, 8192]); w_down = din("w_down", [8192, 2048])
    ident_bf_d = din("ident_bf", [128, 128], BF16); ident_f_d = din("ident_f", [128, 128])
    kcol_d = din("kcol", [128, 1]); maskc_d = din("maskc", [128, 4]); padm_d = din("padmask", [128, 1])
    e_cur_d = din("e_cur", [128, 16, 128], BF16); e_prev_d = din("e_prev", [128, 16, 128], BF16)
    e_new_d = din("e_new", [128, 16, 128], BF16); e_cache_d = din("e_cache", [128, 16, 8], BF16)

    y_own = dout("y_own", [NOWN * 128, 2048])
    kv_last = dout("kv_last", [2, 128, 256])
    ksamp = dout("ksamp", [16, 128, 256]); vsamp = dout("vsamp", [16, 128, 256])
    ssm_fin = dout("ssm_fin", [128, 2, 32])
    ssm_samp = dout("ssm_samp", [16, 2, 4096])
    ca_scr = nc.dram_tensor("ca_scr", [8, 128, 4 * 17 * 2 * 32], BF16, kind="Internal").ap()
    dbg_out = {k: dout("dbg_" + k, shp, dt) for k, (shp, dt) in dbg.items()}
    final_toks = []

    banks = [es.enter_context(nc.psum_tensor("bank%d" % i, [128, 512], F32)) for i in range(8)]
    bkb = [Buf("bank%d" % i, excl=True) for i in range(8)]

    ident = T(es, "ident", [128, 128], BF16); identf = T(es, "identf", [128, 128])
    negpi = T(es, "negpi", [128, 1]); epsc = T(es, "epsc", [128, 1]); ones_bf = T(es, "ones", [128, 64], BF16)
    gcolA = T(es, "gcolA", [128, 16]); gcolM = T(es, "gcolM", [128, 16]); gfin = T(es, "gfin", [128, 2048])
    Dcol = T(es, "Dcol", [128, 8]); bglu = T(es, "bglu", [128, 8]); esink = T(es, "esink", [128, 16])
    maskc = T(es, "maskc", [128, 4]); kcol = T(es, "kcol", [128, 1]); padm = T(es, "padm", [128, 1])
    Ecur = T(es, "Ecur", [128, 16, 128], BF16); Eprev = T(es, "Eprev", [128, 16, 128], BF16)
    Enew = T(es, "Enew", [128, 16, 128], BF16); Ecache = T(es, "Ecache", [128, 16, 8], BF16)
    zT = T(es, "zT", [128, 8, 1152], BF16)

    dma(SP, ident[:], ident_bf_d, [], [ident], None)
    dma(SP, kcol[:], kcol_d, [], [kcol], None)
    dma(SP, gcolA[:], g_attn.rearrange("(k p) -> p k", p=128), [], [gcolA], None, slow=True)
    onesc = T(es, "onesc", [128, 1]); memset(DVE, onesc[:], 1.0, [onesc])
    memset(DVE, negpi[:], -math.pi, [negpi]); memset(DVE, epsc[:], 1e-5, [epsc]); memset(DVE, ones_bf[:], 1.0, [ones_bf])

    def sin_reduce(th, thi, b_th, b_thi):
        ts(DVE, thi, th, 1.0 / TWO_PI, None, ALU.mult, None, [b_th], [b_thi])
        thf = thi.bitcast(F32)
        cp(DVE, thf, thi, [b_thi], [b_thi])
        stt(DVE, th, thf, -TWO_PI, th, ALU.mult, ALU.add, [b_thi, b_th], [b_th])
        ts(DVE, th, th, -3.1415925, 3.1415925, ALU.max, ALU.min, [b_th], [b_th])
        act(th, th, AF.Sin, [b_th], [b_th])

    def rms_stats(x_ap, junk_ap, ss, rstd, R, W_junk):
        act(junk_ap, x_ap, AF.Square, R, [W_junk, ss], accum=ss[:])
        act(rstd[:], ss[:], AF.Ln, [ss, epsc], [rstd], bias=epsc[:], scale=1.0 / 2048)
        act(rstd[:], rstd[:], AF.Exp, [rstd], [rstd], scale=-0.5)

    sA = contextlib.ExitStack()
    KS = list(range(17)) + [128, -8]
    NK = len(KS)
    aH = T(sA, "aH", [128, 2, 32]); ldtH = T(sA, "ldtH", [128, 32]); dtH = T(sA, "dtH", [128, 32])
    lamH = T(sA, "lamH", [128, 2, 32]); AK = T(sA, "AK", [128, NK, 2, 32])
    bH = T(sA, "bH", [128, 2, 32, 16]); cH = T(sA, "cH", [128, 2, 32, 16])
    BbH = T(sA, "BbH", [128, 2, 32, 16]); fH = T(sA, "fH", [128, 2, 32])
    ApowT = T(sA, "ApowT", [128, 32, 2, 128], BF16)
    Hst = T(sA, "Hst", [128, 2, 32])
    BbBD = T(sA, "BbBD", [128, 32, 2, 32], BF16)
    for two in range(2):
        ps = slice(64 * two, 64 * two + 64)
        for ri, src in enumerate((a_re, a_im)):
            dma(SP, aH[ps, ri, :], src.rearrange("(pair two) p -> two p pair", two=2)[two], [], [aH], None, slow=True)
        dma(SP, ldtH[ps, :], bass.AP(tensor=log_dt.tensor, offset=two, ap=[[0, 64], [2, 32]]), [], [ldtH], None, slow=True)
        for ri, src in enumerate((b_re, b_im)):
            dma(SP, bH[ps, ri, :, :], src.rearrange("(pair two) p c -> two p pair c", two=2)[two], [], [bH], None)
    act(dtH[:], ldtH[:], AF.Exp, [ldtH], [dtH])
    tt(DVE, lamH[:], aH[:], dtH[:, None, :].broadcast_to([128, 2, 32]), ALU.mult, [aH, dtH], [lamH])
    sA1 = contextlib.ExitStack()
    mag = T(sA1, "mag", [128, NK, 32]); ang = T(sA1, "ang", [128, NK, 2, 32]); angi = T(sA1, "angi", [128, NK, 2, 32], I32)
    tmpa = T(sA1, "tmpa", [128, 4, 32]); tb = T(sA1, "tb", [128, 2, 32, 16])
    big0 = T(sA1, "big0", [128, 4096]); big1 = T(sA1, "big1", [128, 4096]); big2 = T(sA1, "big2", [128, 4096])
    bigI = T(sA1, "bigI", [128, 4096], I32); dtT = T(sA1, "dtT", [128, 64])
    for i, k in enumerate(KS):
        act(mag[:, i, :], lamH[:, 0, :], AF.Exp, [lamH], [mag], scale=float(k))
        ts(DVE, ang[:, i, 0, :], lamH[:, 1, :], float(k), math.pi / 2, ALU.mult, ALU.add, [lamH], [ang])
        ts(DVE, ang[:, i, 1, :], lamH[:, 1, :], float(k), None, ALU.mult, None, [lamH], [ang])
    sin_reduce(ang[:], angi[:], ang.b, angi.b)
    tt(DVE, AK[:], ang[:], mag[:, :, None, :].broadcast_to([128, NK, 2, 32]), ALU.mult, [ang, mag], [AK])
    i1 = KS.index(1)
    ts(DVE, tmpa[:, 0, :], AK[:, i1, 0, :], -1.0, None, ALU.add, None, [AK], [tmpa])
    tt(DVE, tmpa[:, 1, :], aH[:, 0, :], aH[:, 0, :], ALU.mult, [aH], [tmpa])
    tt(DVE, tmpa[:, 2, :], aH[:, 1, :], aH[:, 1, :], ALU.mult, [aH], [tmpa])
    tt(DVE, tmpa[:, 1, :], tmpa[:, 1, :], tmpa[:, 2, :], ALU.add, [tmpa], [tmpa])
    P.op(DVE, lambda e: e.reciprocal(out=tmpa[:, 1, :], in_=tmpa[:, 1, :]), [tmpa.b], [tmpa.b])
    tt(DVE, tmpa[:, 2, :], tmpa[:, 0, :], aH[:, 0, :], ALU.mult, [tmpa, aH], [tmpa])
    tt(DVE, tmpa[:, 3, :], AK[:, i1, 1, :], aH[:, 1, :], ALU.mult, [AK, aH], [tmpa])
    tt(DVE, tmpa[:, 2, :], tmpa[:, 2, :], tmpa[:, 3, :], ALU.add, [tmpa], [tmpa])
    tt(DVE, fH[:, 0, :], tmpa[:, 2, :], tmpa[:, 1, :], ALU.mult, [tmpa], [fH])
    tt(DVE, tmpa[:, 2, :], AK[:, i1, 1, :], aH[:, 0, :], ALU.mult, [AK, aH], [tmpa])
    tt(DVE, tmpa[:, 3, :], tmpa[:, 0, :], aH[:, 1, :], ALU.mult, [tmpa, aH], [tmpa])
    tt(DVE, tmpa[:, 2, :], tmpa[:, 2, :], tmpa[:, 3, :], ALU.subtract, [tmpa], [tmpa])
    tt(DVE, fH[:, 1, :], tmpa[:, 2, :], tmpa[:, 1, :], ALU.mult, [tmpa], [fH])
    frb = fH[:, 0, :, None].broadcast_to([128, 32, 16]); fib = fH[:, 1, :, None].broadcast_to([128, 32, 16])
    tt(DVE, tb[:, 0], bH[:, 0], frb, ALU.mult, [bH, fH], [tb]); tt(DVE, tb[:, 1], bH[:, 1], fib, ALU.mult, [bH, fH], [tb])
    tt(DVE, BbH[:, 0], tb[:, 0], tb[:, 1], ALU.subtract, [tb], [BbH])
    tt(DVE, tb[:, 0], bH[:, 1], frb, ALU.mult, [bH, fH], [tb]); tt(DVE, tb[:, 1], bH[:, 0], fib, ALU.mult, [bH, fH], [tb])
    tt(DVE, BbH[:, 1], tb[:, 0], tb[:, 1], ALU.add, [tb], [BbH])
    memset(DVE, BbBD[:], 0.0, [BbBD])
    for half in range(2):
        for ri in range(2):
            cp(DVE, BbBD[64 * half:64 * half + 64, :, ri, 16 * half:16 * half + 16], BbH[64 * half:64 * half + 64, ri, :, :], [BbH], [BbBD])
    dma(SP, big0[:], a_re.rearrange("g p -> (g p)").partition_broadcast(128), [], [big0], "big0")
    dma(SP, big1[:], a_im.rearrange("g p -> (g p)").partition_broadcast(128), [], [big1], "big1")
    dma(SP, dtT[:], log_dt.partition_broadcast(128), [], [dtT], "big2")
    dma(SP, identf[:], ident_f_d, [], [identf], None)
    dma(SP, maskc[:], maskc_d, [], [maskc], None); dma(SP, padm[:], padm_d, [], [padm], None)
    dma(SP, gcolM[:], g_mlp.rearrange("(k p) -> p k", p=128), [], [gcolM], None, slow=True)
    dma(SP, Dcol[:], ssm_d.rearrange("(k p) -> p k", p=128), [], [Dcol], None, slow=True)
    dma(SP, bglu[:], b_glu.rearrange("(k p) -> p k", p=128), [], [bglu], None, slow=True)
    dma(SP, gfin[:], g_fin.partition_broadcast(128), [], [gfin], None)
    dma(SP, esink[:], sinks_p.partition_broadcast(128), [], [esink], None)
    dma(SP, Ecur[:], e_cur_d, [], [Ecur], None); dma(SP, Eprev[:], e_prev_d, [], [Eprev], None)
    dma(SP, Enew[:], e_new_d, [], [Enew], None); dma(SP, Ecache[:], e_cache_d, [], [Ecache], None)
    act(esink[:], esink[:], AF.Exp, [esink], [esink])
    for two in range(2):
        ps = slice(64 * two, 64 * two + 64)
        for ri, src in enumerate((c_re, c_im)):
            v = src.rearrange("(pair two) c p -> two p pair c", two=2)[two]
            for pr in range(32):
                dma(SP, cH[ps, ri, pr, :], v[:, pr, :], [], [cH], None, slow=True)
    act(dtT[:], dtT[:], AF.Exp, [dtT], [dtT])
    dtb = dtT[:, :, None].broadcast_to([128, 64, 64])
    tt(DVE, big0[:].rearrange("s (g p) -> s g p", p=64), big0[:].rearrange("s (g p) -> s g p", p=64), dtb, ALU.mult, [big0, dtT], [big0])
    tt(DVE, big1[:].rearrange("s (g p) -> s g p", p=64), big1[:].rearrange("s (g p) -> s g p", p=64), dtb, ALU.mult, [big1, dtT], [big1])
    act(big0[:], big0[:], AF.Exp, [big0, kcol], [big0], scale=kcol[:])
    for ri, off in ((0, math.pi / 2), (1, 0.0)):
        ts(DVE, big2[:], big1[:], kcol[:], off, ALU.mult, ALU.add, [big1, kcol], [big2])
        sin_reduce(big2[:], bigI[:], big2.b, bigI.b)
        tt(DVE, ApowT[:, :, ri, :], big2[:].rearrange("s (a q) -> s a q", q=128), big0[:].rearrange("s (a q) -> s a q", q=128), ALU.mult, [big2, big0], [ApowT])
    free_scope(sA1, [mag, ang, angi, tmpa, tb, big0, big1, big2, bigI, dtT])

    NJ = 80
    sU = contextlib.ExitStack()
    uTd = T(sU, "uTd", [128, 8, 16, NJ], BF16)
    sBC = contextlib.ExitStack()
    Wu = T(sBC, "Wu", [128, 16, 1024], BF16)
    for k4 in range(4):
        dma(POOL, Wu[:, 4 * k4:4 * k4 + 4, :], w_in[512 * k4:512 * (k4 + 1), 1536:2560].rearrange("(k p) c -> p k c", p=128), [], [Wu], "Wu")
    for k in range(16):
        ts(DVE, Wu[:, k, :], Wu[:, k, :], gcolA[:, k:k + 1], None, ALU.mult, None, [Wu, gcolA], [Wu])
    sB = contextlib.ExitStack()
    T1 = T(sB, "T1", [128, 32, 2, 16]); T2 = T(sB, "T2", [128, 32, 2, 16])
    cp(DVE, T1[:, :, 0, :], BbH[:, 0], [BbH], [T1]); ts(DVE, T1[:, :, 1, :], BbH[:, 1], -1.0, None, ALU.mult, None, [BbH], [T1])
    cp(DVE, T2[:, :, 0, :], BbH[:, 1], [BbH], [T2]); cp(DVE, T2[:, :, 1, :], BbH[:, 0], [BbH], [T2])
    xb = [T(sB, "xb%d" % i, [128, 2048], BF16) for i in range(3)]
    junk = T(sB, "junk", [128, 2048], BF16)
    ssq = [T(sB, "ss%d" % i, [128, 1]) for i in range(3)]; rstd = [T(sB, "rstd%d" % i, [128, 1]) for i in range(3)]
    xT = [T(sB, "xT%d" % i, [128, 16, 128], BF16) for i in range(3)]
    usb = [T(sB, "usb%d" % i, [128, 2, 1024], BF16) for i in range(2)]
    prod = T(sB, "prod", [128, 2, 32, 2, 2, 16]); Stile = [T(sB, "Stile%d" % i, [128, 2, 32]) for i in range(2)]
    m12 = T(sB, "m12", [128, 2, 2, 32]); e12 = T(sB, "e12", [128, 2, 32])
    memset(POOL, Hst[:], 0.0, [Hst])

    def cstep(eng, H, Pidx, S_ap, S_tok, m12, e12):
        Pr = AK[:, Pidx, 0, :]; Pi = AK[:, Pidx, 1, :]
        tt(eng, m12[:, 0], H[:], Pr[:, None, :].broadcast_to([128, 2, 32]), ALU.mult, [H, AK], [m12])
        tt(eng, m12[:, 1], H[:], Pi[:, None, :].broadcast_to([128, 2, 32]), ALU.mult, [H, AK], [m12])
        tt(eng, e12[:, 0, :], m12[:, 0, 0, :], m12[:, 1, 1, :], ALU.subtract, [m12], [e12])
        tt(eng, e12[:, 1, :], m12[:, 0, 1, :], m12[:, 1, 0, :], ALU.add, [m12], [e12])
        tt(eng, H[:], e12[:], S_ap, ALU.add, [e12, S_tok], [H])

    iP128 = KS.index(128)
    NS = 3

    def pre_A(t):
        s = t % NS
        dma(POOL, xb[s][:], xprev[128 * t:128 * (t + 1), :], [], [xb[s]], "xb%d" % s)
        rms_stats(xb[s][:], junk[:], ssq[s], rstd[s], [xb[s]], junk)
        for hb in range(2):
            pst = banks[hb][:].bitcast(BF16)
            for j in range(8):
                k = hb * 8 + j
                tr(pst[:, 128 * j:128 * (j + 1)], xb[s][:, 128 * k:128 * (k + 1)], ident[:], [xb[s], ident], [bkb[hb]])
            cp(ACT, xT[s][:, 8 * hb:8 * hb + 8, :].rearrange("p k t -> p (k t)"), pst, [bkb[hb]], [xT[s]])

    def pre_B(t):
        s = t % NS
        ub = usb[(t // 2) % 2]
        for nh in range(2):
            for k in range(16):
                mm(banks[2 + nh][:], xT[s][:, k, :], Wu[:, k, 512 * nh:512 * (nh + 1)], k == 0, k == 15, [xT[s], Wu], [bkb[2 + nh]])
            act(ub[:, t % 2, 512 * nh:512 * (nh + 1)], banks[2 + nh][:], AF.Copy, [bkb[2 + nh], rstd[s]], [ub], scale=rstd[s][:])

    def pre_C(tl):
        nt = len(tl)
        ub = usb[(tl[0] // 2) % 2]
        for q in range(4):
            for hb_ in range(2):
                bk = 4 + 2 * (q % 2) + hb_
                vb = banks[bk][:].rearrange("p (a r c) -> p a r c", a=4, r=2)
                for a in range(4):
                    pair = 8 * q + 4 * hb_ + a
                    for ri in range(2):
                        mm(vb[:, a, ri, 0:32 * nt], ApowT[:, pair, ri, :], ub[:, 0:nt, 32 * pair:32 * (pair + 1)], True, True, [ApowT, ub], [bkb[bk]])
                pa = slice(8 * q + 4 * hb_, 8 * q + 4 * hb_ + 4)
                for half in range(2):
                    ps = slice(64 * half, 64 * half + 64)
                    vsel = vb[ps].rearrange("p a r (t c) -> p (a r) t c", t=2)[:, :, 0:nt, 16 * half:16 * half + 16]
                    for x_, Tx in enumerate((T1, T2)):
                        tb_ = Tx[ps, pa].rearrange("p a r c -> p (a r) c")[:, :, None, :].broadcast_to([64, 8, nt, 16])
                        tt(DVE, prod[ps, x_, pa].rearrange("p a r t c -> p (a r) t c")[:, :, 0:nt, :], vsel, tb_, ALU.mult, [bkb[bk], Tx], [prod])
        for ti in range(nt):
            P.op(DVE, lambda e, ti=ti: e.tensor_reduce(out=Stile[ti][:].rearrange("p x a -> p (x a)"), in_=prod[:].rearrange("p x a r t c -> p (x a) r t c")[:, :, :, ti, :], axis=AX.XY, op=ALU.add), [prod.b], [Stile[ti].b])
            cstep(DVE, Hst, iP128, Stile[ti][:], Stile[ti], m12, e12)

    pend = []
    for it in range(npre + 2):
        if it < npre:
            pre_A(it)
        if 0 <= it - 1 < npre:
            pre_B(it - 1)
            pend.append(it - 1)
        if len(pend) >= 3 or (it == npre + 1 and pend):
            tl = pend[:2] if (pend[0] % 2 == 0 and len(pend) >= 2) else pend[:1]
            pre_C(tl)
            pend = pend[len(tl):]
    while pend:
        tl = pend[:2] if (pend[0] % 2 == 0 and len(pend) >= 2) else pend[:1]
        pre_C(tl)
        pend = pend[len(tl):]
    if "hin" in dbg:
        dma(SP, dbg_out["hin"], Hst[:], [Hst], [], "dbg"); final_toks.append(Hst.b)
    free_scope(sB, [T1, T2, junk, m12, e12, prod] + Stile + xb + ssq + rstd + xT + usb)

    memset(DVE, uTd[:, :, 0:8, 64:80], 0.0, [uTd])
    sC1 = contextlib.ExitStack()
    xbC = [T(sC1, "xbC%d" % i, [128, 2048], BF16) for i in range(2)]
    xnC = [T(sC1, "xnC%d" % i, [128, 2048], BF16) for i in range(2)]
    junkC = T(sC1, "junkC", [128, 2048], BF16)
    ssC = [T(sC1, "ssC%d" % i, [128, 1]) for i in range(2)]; rsC = [T(sC1, "rsC%d" % i, [128, 1]) for i in range(2)]
    hT4 = T(sC1, "hT4", [128, 16, 4, 128], BF16)
    cnt = 0
    for batch in ((0, 1, 2, 3), (4, 5, 6, 7), (8,)):
        nb = len(batch)
        for ti, tile in enumerate(batch):
            s = cnt % 2; cnt += 1
            dma(POOL, xbC[s][:], xown[128 * tile:128 * (tile + 1), :], [], [xbC[s]], "xbC%d" % s)
            rms_stats(xbC[s][:], junkC[:], ssC[s], rsC[s], [xbC[s]], junkC)
            ts(DVE, xnC[s][:], xbC[s][:], rsC[s][:], None, ALU.mult, None, [xbC[s], rsC[s]], [xnC[s]])
            for hb in range(2):
                pst = banks[hb][:].bitcast(BF16)
                for j in range(8):
                    k = hb * 8 + j
                    tr(pst[:, 128 * j:128 * (j + 1)], xnC[s][:, 128 * k:128 * (k + 1)], ident[:], [xnC[s], ident], [bkb[hb]])
                cp(ACT, hT4[:, 8 * hb:8 * hb + 8, ti, :], pst.rearrange("p (k t) -> p k t", t=128), [bkb[hb]], [hT4])
        for ct in range(8):
            bk = 2 + ct % 2
            for k in range(16):
                mm(banks[bk][:, 0:128 * nb], Wu[:, k, 128 * ct:128 * (ct + 1)], hT4[:, k, 0:nb, :], k == 0, k == 15, [Wu, hT4], [bkb[bk]])
            if nb == 4:
                j0 = 8 * batch[0]
                cp(DVE if ct % 2 else ACT, uTd[:, ct, :, j0:j0 + 32].rearrange("p t j -> p j t"), banks[bk][:].rearrange("p (j t) -> p j t", t=16), [bkb[bk]], [uTd])
            else:
                cp(DVE if ct % 2 else ACT, uTd[:, ct, 8:16, 64:80].rearrange("p t j -> p j t"), banks[bk][:, 0:128].rearrange("p (j t) -> p j t", t=8), [bkb[bk]], [uTd])
    free_scope(sC1, xbC + xnC + [junkC] + ssC + rsC + [hT4])
    free_scope(sBC, [Wu])
    sC = contextlib.ExitStack()
    S_H = T(sC, "S_H", [128, 32, 2, NJ])
    Hprev = T(sC, "Hprev", [128, 32, 2, NJ], BF16)
    Hv = T(sC, "Hv", [128, 32, 2, 16])
    sC0 = contextlib.ExitStack()
    h0n = T(sC0, "h0n", [16, 2, 2048]); h0H = T(sC0, "h0H", [128, 32, 2, 16]); hvt = T(sC0, "hvt", [128, 4, 32, 16])
    for q in range(2):
        dma(SP, h0n[:, 0, :], st_re[:, 2048 * q:2048 * (q + 1)], [], [h0n], "h0n")
        dma(SP, h0n[:, 1, :], st_im[:, 2048 * q:2048 * (q + 1)], [], [h0n], "h0n")
        for ri in range(2):
            pb = banks[ri][:].rearrange("p (a b) -> p a b", b=16)
            for a in range(16):
                tr(pb[:, a, :], h0n[:, ri, 128 * a:128 * (a + 1)], identf[0:16, 0:16], [h0n, identf], [bkb[ri]])
            cp(ACT, h0H[:, 16 * q:16 * q + 16, ri, :], pb[:, 0:16, :], [bkb[ri]], [h0H])
    im8 = KS.index(-8)
    Ar8 = AK[:, im8, 0, :, None].broadcast_to([128, 32, 16]); Ai8 = AK[:, im8, 1, :, None].broadcast_to([128, 32, 16])
    tt(DVE, hvt[:, 0], h0H[:, :, 0, :], Ar8, ALU.mult, [h0H, AK], [hvt]); tt(DVE, hvt[:, 1], h0H[:, :, 1, :], Ai8, ALU.mult, [h0H, AK], [hvt])
    tt(DVE, hvt[:, 2], h0H[:, :, 1, :], Ar8, ALU.mult, [h0H, AK], [hvt]); tt(DVE, hvt[:, 3], h0H[:, :, 0, :], Ai8, ALU.mult, [h0H, AK], [hvt])
    tt(DVE, Hv[:, :, 0, :], hvt[:, 0], hvt[:, 1], ALU.subtract, [hvt], [Hv]); tt(DVE, Hv[:, :, 1, :], hvt[:, 2], hvt[:, 3], ALU.add, [hvt], [Hv])
    cp(DVE, Hprev[:, :, :, 64:80], Hv[:], [Hv], [Hprev])
    free_scope(sC0, [h0n, h0H, hvt])


    sC2 = contextlib.ExitStack()
    MBp = T(sC2, "MBp", [128, 4, 17, 2, 32], BF16)
    CAp = T(sC2, "CAp", [128, 4, 17, 2, 32], BF16)
    WSp = T(sC2, "WSp", [128, 16, 2, 128], BF16)
    tq = T(sC2, "tq", [128, 4, 4, 17, 16])
    Kd = T(sC2, "Kd", [128, 16, 32], BF16); Kfull = T(sC2, "Kfull", [128, 16, 4, 32], BF16)
    yv = T(sC2, "yv", [128, 16, NJ]); y2 = T(sC2, "y2", [128, 16, NJ])
    Hcur = T(sC2, "Hcur", [128, 2, 32]); Hcur2 = T(sC2, "Hcur2", [128, 2, 32])
    m12c = T(sC2, "m12c", [128, 2, 2, 32]); e12c = T(sC2, "e12c", [128, 2, 32])
    memset(DVE, MBp[:], 0.0, [MBp]); memset(DVE, CAp[:], 0.0, [CAp])
    ca_tok = Buf("ca_scr")

    def bd_table(eng, dst, X, ct, neg_im):
        pa = slice(4 * ct, 4 * ct + 4)
        Ar = AK[:, 0:17, 0, pa].rearrange("p k a -> p a k")[:, :, :, None].broadcast_to([128, 4, 17, 16])
        Ai = AK[:, 0:17, 1, pa].rearrange("p k a -> p a k")[:, :, :, None].broadcast_to([128, 4, 17, 16])
        Xr = X[:, 0, pa, None, :].broadcast_to([128, 4, 17, 16]); Xi = X[:, 1, pa, None, :].broadcast_to([128, 4, 17, 16])
        tt(eng, tq[:, 0], Ar, Xr, ALU.mult, [AK, X], [tq]); tt(eng, tq[:, 1], Ai, Xi, ALU.mult, [AK, X], [tq])
        tt(eng, tq[:, 2], Ai, Xr, ALU.mult, [AK, X], [tq]); tt(eng, tq[:, 3], Ar, Xi, ALU.mult, [AK, X], [tq])
        for half in range(2):
            ps = slice(64 * half, 64 * half + 64); cs = slice(16 * half, 16 * half + 16)
            tt(eng, dst[ps, :, :, 0, cs], tq[ps, 0], tq[ps, 1], ALU.subtract, [tq], [dst])
            if neg_im:
                stt(eng, dst[ps, :, :, 1, cs], tq[ps, 2], -1.0, tq[ps, 3], ALU.mult, ALU.subtract, [tq], [dst])
            else:
                tt(eng, dst[ps, :, :, 1, cs], tq[ps, 2], tq[ps, 3], ALU.add, [tq], [dst])

    bd_table(DVE, MBp, BbH, 0, False)
    for ct in range(8):
        for rnd in range(4):
            for kk in range(4):
                k = 4 * rnd + kk
                for ri in range(2):
                    bk = (kk * 2 + ri) // 4; col = ((kk * 2 + ri) % 4) * 128
                    for pl in range(4):
                        mm(banks[bk][32 * pl:32 * pl + 32, col:col + 128], MBp[:, pl, k, ri, :], ident[:], True, True, [MBp, ident], [bkb[bk]], tp=(0, 32 * pl))
            for bk in range(2):
                cp(ACT, WSp[:, 4 * rnd + 2 * bk:4 * rnd + 2 * bk + 2, :, :].rearrange("p k r q -> p (k r q)"), banks[bk][:], [bkb[bk]], [WSp])
        bd_table(DVE, CAp, cH, ct, True)
        dma(SP, ca_scr[ct], CAp[:].rearrange("p a k r c -> p (a k r c)"), [CAp], [ca_tok], "ca_out")
        if ct < 7:
            bd_table(DVE, MBp, BbH, ct + 1, False)
        for pl in range(4):
            bk = 2 + pl // 2
            sps = banks[bk][:, 0:2 * 2 * NJ].rearrange("p (a r j) -> p a r j", a=2, r=2)
            for two in range(2):
                for ri in range(2):
                    for s_ in range(16):
                        mm(sps[64 * two:64 * two + 64, pl % 2, ri, :], WSp[32 * pl:32 * pl + 32, 15 - s_, ri, 64 * two:64 * two + 64],
                           uTd[32 * pl:32 * pl + 32, ct, s_, :], s_ == 0, s_ == 15, [WSp, uTd], [bkb[bk]], tp=(32 * pl, 64 * two))
            if pl % 2 == 1:
                cp(ACT, S_H[:, 4 * ct + pl - 1:4 * ct + pl + 1, :, :], sps, [bkb[bk]], [S_H])

    i16 = KS.index(16)
    Hc = [Hcur, Hcur2]
    cp(DVE, Hc[0][:], Hst[:], [Hst], [Hc[0]])
    P16 = AK[:, i16]
    for j in range(64):
        a_, b_ = Hc[j % 2], Hc[(j + 1) % 2]
        cp(ACT, Hprev[:, :, :, j].rearrange("p a r -> p r a"), a_[:], [a_], [Hprev])
        tt(DVE, m12c[:], a_[:, None, :, :].broadcast_to([128, 2, 2, 32]), P16[:, :, None, :].broadcast_to([128, 2, 2, 32]), ALU.mult, [a_, AK], [m12c])
        tt(DVE, e12c[:, 0, :], m12c[:, 0, 0, :], m12c[:, 1, 1, :], ALU.subtract, [m12c], [e12c])
        tt(DVE, e12c[:, 1, :], m12c[:, 0, 1, :], m12c[:, 1, 0, :], ALU.add, [m12c], [e12c])
        tt(DVE, b_[:], e12c[:], S_H[:, :, :, j].rearrange("p a r -> p r a"), ALU.add, [e12c, S_H], [b_])
    dma(SP, ssm_fin, Hcur[:], [Hcur], [], "fin_out"); final_toks.append(Hcur.b)
    sC3 = contextlib.ExitStack()
    hft = T(sC3, "hft", [128, 4, 32, 16]); HfS = T(sC3, "HfS", [128, 32, 2, 16]); hfo = [T(sC3, "hfo%d" % i, [16, 512]) for i in range(2)]
    Ar16 = AK[:, i16, 0, :, None].broadcast_to([128, 32, 16]); Ai16 = AK[:, i16, 1, :, None].broadcast_to([128, 32, 16])
    tt(DVE, hft[:, 0], Hv[:, :, 0, :], Ar16, ALU.mult, [Hv, AK], [hft]); tt(DVE, hft[:, 1], Hv[:, :, 1, :], Ai16, ALU.mult, [Hv, AK], [hft])
    tt(DVE, hft[:, 2], Hv[:, :, 1, :], Ar16, ALU.mult, [Hv, AK], [hft]); tt(DVE, hft[:, 3], Hv[:, :, 0, :], Ai16, ALU.mult, [Hv, AK], [hft])
    tt(DVE, hft[:, 0], hft[:, 0], hft[:, 1], ALU.subtract, [hft], [hft]); tt(DVE, hft[:, 2], hft[:, 2], hft[:, 3], ALU.add, [hft], [hft])
    tt(DVE, HfS[:, :, 0, :], hft[:, 0], S_H[:, :, 0, 64:80], ALU.add, [hft, S_H], [HfS])
    tt(DVE, HfS[:, :, 1, :], hft[:, 2], S_H[:, :, 1, 64:80], ALU.add, [hft, S_H], [HfS])
    nst = 0
    for q in range(2):
        for ri in range(2):
            for a4 in range(4):
                bk = 4 + a4
                for a in range(4):
                    pair = 16 * q + 4 * a4 + a
                    tr(banks[bk][0:16, 128 * a:128 * (a + 1)], HfS[:, pair, ri, :], identf[:], [HfS, identf], [bkb[bk]])
                st = hfo[nst % 2]; nst += 1
                cp(ACT, st[:], banks[bk][0:16, :], [bkb[bk]], [st])
                c0_ = 2048 * q + 512 * a4
                dma(SP, ssm_samp[:, ri, c0_:c0_ + 512], st[:], [st], [], "hfo_out%d" % (nst % 2))
    final_toks.extend([hfo[0].b, hfo[1].b])
    CAbuf = [CAp, MBp]
    dma(SP, CAbuf[0][:].rearrange("p a k r c -> p (a k r c)"), ca_scr[0], [ca_tok], [CAbuf[0]], "ca_in0")
    for ct in range(8):
        CAp = CAbuf[ct % 2]
        for pl in range(4):
            for ri in range(2):
                mm(banks[0][32 * pl:32 * pl + 32, :], BbBD[:, 4 * ct + pl, ri, :], CAp[:, pl, 0:16, ri, :], ri == 0, ri == 1, [BbBD, CAp], [bkb[0]], tp=(0, 32 * pl))
        if ct < 7:
            nb_ = CAbuf[(ct + 1) % 2]
            dma(SP, nb_[:].rearrange("p a k r c -> p (a k r c)"), ca_scr[ct + 1], [ca_tok], [nb_], "ca_in%d" % ((ct + 1) % 2))
        cp(ACT, Kd[:].rearrange("p t c -> p (t c)"), banks[0][:], [bkb[0]], [Kd])
        tt(DVE, Kfull[:], Kd[:, :, None, :].broadcast_to([128, 16, 4, 32]), maskc[:, None, :, None].broadcast_to([128, 16, 4, 32]), ALU.mult, [Kd, maskc], [Kfull])
        ybk = [(1, 0, 6), (2, 6, 12), (3, 12, 16)]
        for bk, t0, t1 in ybk:
            yps = banks[bk][:, 0:(t1 - t0) * NJ].rearrange("p (t j) -> p t j", j=NJ)
            for tpr in range(t0, t1):
                for tau in range(tpr + 1):
                    mm(yps[:, tpr - t0, :], Kfull[:, tau].rearrange("p a c -> p (a c)"), uTd[:, ct, tpr - tau, :], tau == 0, False, [Kfull, uTd], [bkb[bk]])
                for pl in range(4):
                    for ri in range(2):
                        mm(yps[32 * pl:32 * pl + 32, tpr - t0, :], CAp[:, pl, tpr + 1, ri, :], Hprev[:, 4 * ct + pl, ri, :], False, ri == 1,
                           [CAp, Hprev], [bkb[bk]], tp=(0, 32 * pl))
            stt(DVE, yv[:, t0:t1, :], uTd[:, ct, t0:t1, :], Dcol[:, ct:ct + 1], yps, ALU.mult, ALU.add, [uTd, Dcol, bkb[bk]], [yv])
        act(y2[:], yv[:], AF.Square, [yv], [y2])
        act(y2[:], y2[:], AF.Identity, [y2, onesc], [y2], bias=onesc[:], scale=0.044715)
        tt(DVE, y2[:], y2[:], yv[:], ALU.mult, [y2, yv], [y2])
        act(y2[:], y2[:], AF.Sigmoid, [y2], [y2], scale=1.5957691216057308)
        tt(DVE, zT[:, ct, 0:1024].rearrange("p (j t) -> p t j", t=16), yv[:, :, 0:64], y2[:, :, 0:64], ALU.mult, [yv, y2], [zT])
        tt(DVE, zT[:, ct, 1024:1152].rearrange("p (b i) -> p i b", i=8), yv[:, 8:16, 64:80], y2[:, 8:16, 64:80], ALU.mult, [yv, y2], [zT])
    if "zT" in dbg:
        dma(SP, dbg_out["zT"], zT[:], [zT], [], "dbg"); final_toks.append(zT.b)
    if "uTd" in dbg:
        dma(SP, dbg_out["uTd"], uTd[:], [uTd], [], "dbg"); final_toks.append(uTd.b)
    if "S_H" in dbg:
        dma(SP, dbg_out["S_H"], S_H[:], [S_H], [], "dbg"); final_toks.append(S_H.b)
    free_scope(sC3, [hft, HfS] + hfo)
    free_scope(sC2, CAbuf + [WSp, tq, Kd, Kfull, yv, y2, Hcur, Hcur2, m12c, e12c])
    free_scope(sC, [S_H, Hprev, Hv])
    free_scope(sU, [uTd])
    free_scope(sA, [aH, ldtH, dtH, lamH, AK, bH, cH, BbH, fH, ApowT, Hst, BbBD])

    if dbg.get("stop_after_C"):
        P.emit(final_bufs=final_toks)
        es.close()
        return nc, P

    class V:
        def __init__(self, ap, name):
            self.ap = ap
            self.b = Buf(name, hazards)

        def __getitem__(self, k):
            return self.ap[k]

    def alias(new, old):
        if P_plan[0]:
            return
        ev = []
        for o in old:
            ev.extend(o.b.events())
        for n in new:
            for e_ in ev:
                n.b.add_read(e_)

    sD = contextlib.ExitStack()
    slots = [T(sD, "slot%d" % i, [128, 4096], BF16) for i in range(NSLOT)]
    arX = T(sD, "arX", [128, 24576], BF16); arY = T(sD, "arY", [128, 6144], BF16); arZ = T(sD, "arZ", [128, 9216])
    kAB = T(sD, "kAB", [128, 2, 2, 4, 128], BF16); Vtok = T(sD, "Vtok", [128, 4, 256], BF16)
    sT = T(sD, "sT", [128, 8, 384], BF16)
    xbD = [T(sD, "xbD%d" % i, [128, 2048], BF16) for i in range(2)]
    xnD = [T(sD, "xnD%d" % i, [128, 2048], BF16) for i in range(2)]
    ssD = [T(sD, "ssD%d" % i, [128, 1]) for i in range(2)]; rsD = [T(sD, "rsD%d" % i, [128, 1]) for i in range(2)]
    PT = [T(sD, "PT%d" % i, [128, 512], BF16) for i in range(2)]; PTp = T(sD, "PTp", [128, 512], BF16)
    PTm = PT
    rden = T(sD, "rden", [64, 512]); kvo = T(sD, "kvo", [128, 512])
    tmpb = [T(sD, "tmpb%d" % i, [128, 384], BF16) for i in range(2)]
    tmpf = [T(sD, "tmpf%d" % i, [128, 384]) for i in range(2)]
    hT = V(arX[:, 0:6144].rearrange("p (k t) -> p k t", t=384), "hT")
    sgT = V(arX[:, 6144:18432].rearrange("p (c t) -> p c t", t=384), "sgT")
    aT = V(arX[:, 18432:24576].rearrange("p (h t) -> p h t", t=384), "aT")
    actT = V(arX[:, :].rearrange("p (f t) -> p f t", t=384), "actT")
    qT = V(arY[:, 0:3072].rearrange("p (c t) -> p c t", t=384), "qT")
    hmT = V(arY[:, :].rearrange("p (k t) -> p k t", t=384), "hmT")
    mergedT = V(arZ[:, 0:3072].bitcast(BF16).rearrange("p (k t) -> p k t", t=384), "mergedT")
    x1 = V(arZ[:, 3072:9216].rearrange("p (a d) -> p a d", d=2048), "x1")
    zb = arZ[:, :].bitcast(BF16)
    ckb = V(zb[:, 0:4096].rearrange("p (b c) -> p b c", c=256), "ckb")
    cvb = V(zb[:, 4096:8192].rearrange("p (b c) -> p b c", c=256), "cvb")
    kcT = V(zb[:, 8192:12288].rearrange("p (b c w) -> p b c w", b=16, c=2), "kcT")
    PTc = V(zb[:, 12288:14336], "PTc")
    qB = V(zb[:, 14336:15360].rearrange("p (c t) -> p c t", t=128), "qB")

    P_plan = [True]
    cc_tok = Buf("cachecopy")
    plan = []
    wstate = {"next": 0, "issued": 0}

    def wget(dram_ap, nparts, shape):
        nel = int(np.prod(shape))
        if P_plan[0]:
            plan.append((dram_ap, nparts, shape, nel))
            return slots[0], slots[0][0:nparts, 0:nel].rearrange(_fmt(shape), **_kw(shape))
        i = wstate["next"]
        wstate["next"] += 1
        while wstate["issued"] < min(len(plan), i + LOOKAHEAD + 1):
            n = wstate["issued"]
            ap_, np_, shp_, nel_ = plan[n]
            sl = slots[n % NSLOT]
            dma(POOL, sl[0:np_, 0:nel_].rearrange(_fmt(shp_), **_kw(shp_)), ap_, [], [sl], "slot%d" % (n % NSLOT))
            wstate["issued"] += 1
        sl = slots[i % NSLOT]
        return sl, sl[0:nparts, 0:nel].rearrange(_fmt(shape), **_kw(shape))

    def _fmt(shape):
        return "p (a b) -> p a b"

    def _kw(shape):
        return {"b": shape[1]}

    realop, realdma = P.op, P.dma

    class StopD(Exception):
        pass

    def stage(name, g):
        if dbg.get("stopD") == (name, g):
            raise StopD()

    def phaseD():
        rot = [0]

        def obank():
            b = 2 + rot[0] % 3
            rot[0] += 1
            return b

        def load_norm_T(src_ap, s, dstT, col0, gcol, fp32_src=None):
            if fp32_src is None:
                dma(POOL, xbD[s][:], src_ap, [], [xbD[s]], "xbD%d" % s)
                xin, xtok = xbD[s][:], xbD[s]
            else:
                xin, xtok = fp32_src, x1
            xn_ = xnD[s]
            rms_stats(xin, xn_[:], ssD[s], rsD[s], [xtok], xn_)
            ts(DVE, xn_[:], xin, rsD[s][:], None, ALU.mult, None, [xtok, rsD[s]], [xn_])
            for hb in range(2):
                pst = banks[hb][:].bitcast(BF16)
                for j in range(8):
                    k = hb * 8 + j
                    tr(pst[:, 128 * j:128 * (j + 1)], xn_[:, 128 * k:128 * (k + 1)], ident[:], [xn_, ident], [bkb[hb]])
                tt(DVE, dstT[:, 8 * hb:8 * hb + 8, col0:col0 + 128], pst.rearrange("p (k t) -> p k t", t=128),
                   gcol[:, 8 * hb:8 * hb + 8, None].broadcast_to([128, 8, 128]), ALU.mult, [bkb[hb], gcol], [dstT])

        def k_proj(blk, bv, cols, ncol, slot_list):
            for ch in range(2):
                bk = obank()
                for k in range(16):
                    mm(banks[bk][:, 0:ncol], bv[:, k, 128 * ch:128 * ch + 128], hT[:, k, cols], k == 0, k == 15, [blk, hT], [bkb[bk]])
                for i, sl in enumerate(slot_list):
                    cp(ACT, kAB[:, ch, 0, sl, :], banks[bk][:, 128 * i:128 * i + 128], [bkb[bk]], [kAB])
                bk = obank()
                for k in range(16):
                    mm(banks[bk][64:128, 0:ncol], bv[:, k, 128 * ch:128 * ch + 64], hT[:, k, cols], k == 0, k == 15, [blk, hT], [bkb[bk]], tp=(0, 64))
                for k in range(16):
                    mm(banks[bk][0:64, 0:ncol], bv[:, k, 128 * ch + 64:128 * ch + 128], hT[:, k, cols], k == 0, k == 15, [blk, hT], [bkb[bk]], tp=(0, 0))
                for i, sl in enumerate(slot_list):
                    cp(ACT, kAB[:, ch, 1, sl, :], banks[bk][:, 128 * i:128 * i + 128], [bkb[bk]], [kAB])

        def v_proj(blk, bv, col0, slot, fp32_out_cols=None):
            bk = obank()
            for k in range(16):
                mm(banks[bk][:, 0:256], hT[:, k, col0:col0 + 128], bv[:, k, :], k == 0, k == 15, [blk, hT], [bkb[bk]])
            if fp32_out_cols is not None:
                cp(DVE, kvo[:, fp32_out_cols:fp32_out_cols + 256], banks[bk][:, 0:256], [bkb[bk]], [kvo])
                cp(ACT, Vtok[:, slot, :], kvo[:, fp32_out_cols:fp32_out_cols + 256], [kvo], [Vtok])
            elif slot is not None:
                cp(ACT, Vtok[:, slot, :], banks[bk][:, 0:256], [bkb[bk]], [Vtok])

        def attn_norm(kv, col0):
            tt(DVE, rden[:].rearrange("p (h t) -> p h t", t=128), banks[7][0:64, :].rearrange("p (h t) -> p h t", t=128),
               esink[0:64, 4 * kv:4 * kv + 4, None].broadcast_to([64, 4, 128]), ALU.add, [bkb[7], esink], [rden])
            P.op(DVE, lambda e: e.reciprocal(out=rden[:], in_=rden[:]), [rden.b], [rden.b])
            tt(DVE, aT[0:64, 4 * kv:4 * kv + 4, col0:col0 + 128].rearrange("p (c h) t -> p h c t", h=2),
               banks[6][0:64, :].rearrange("p (h c t) -> p h c t", h=2, c=2), rden[:].rearrange("p (h c t) -> p h c t", h=2, c=2),
               ALU.mult, [bkb[6], rden], [aT])

        def scores(kv, slot, col0, bankE, bankO, Etab, pti, perm=False):
            ch, par = kv // 2, kv % 2
            kE = kAB[0:64, ch, par, slot, :]
            kO = kAB[64:128, ch, 1 - par, slot, :]
            mm(banks[bankE][:, 0:256], kE, qT[0:64, 2 * kv:2 * kv + 2, col0:col0 + 128], True, True, [kAB, qT], [bkb[bankE]], tp=(0, 0))
            mm(banks[bankO][:, 0:256], kO, qT[64:128, 2 * kv:2 * kv + 2, col0:col0 + 128], True, True, [kAB, qT], [bkb[bankO]], tp=(64, 0))
            act(PT[pti][:, 0:256], banks[bankE][:, 0:256], AF.Exp, [bkb[bankE]], [PT[pti]])
            act(PT[pti][:, 256:512], banks[bankO][:, 0:256], AF.Exp, [bkb[bankO]], [PT[pti]])
            if not perm:
                tt(DVE, PTm[pti][:], PT[pti][:], Etab[:, 4 * kv:4 * kv + 4, :].rearrange("p h t -> p (h t)"), ALU.mult, [PT[pti], Etab], [PTm[pti]])
            else:
                tt(DVE, PTp[:].rearrange("p (b s i) -> p s b i", b=16, s=4), PT[pti][:].rearrange("p (s b i) -> p s b i", s=4, b=16),
                   Etab[:, 4 * kv:4 * kv + 4, :].rearrange("p s (b i) -> p s b i", i=8), ALU.mult, [PT[pti], Etab], [PTp])

        def attn_norm_sample(kv, col0):
            for h in range(2):
                dv = banks[7][0:64, :].rearrange("p (b h c i) -> p b h c i", b=16, h=2, c=2)[:, :, h]
                ov = banks[6][0:64, :].rearrange("p (b h c i) -> p b h c i", b=16, h=2, c=2)[:, :, h]
                rv = rden[:].rearrange("p (b h c i) -> p b h c i", b=16, h=2, c=2)[:, :, h]
                tt(DVE, rv, dv, esink[0:64, None, 4 * kv + 2 * h:4 * kv + 2 * h + 2, None].broadcast_to([64, 16, 2, 8]), ALU.add, [bkb[7], esink], [rden])
                P.op(DVE, lambda e, rv=rv: e.reciprocal(out=rv, in_=rv), [rden.b], [rden.b])
                tt(DVE, aT[0:64, 4 * kv:4 * kv + 4, col0:col0 + 128].rearrange("p (c h) (b i) -> p h b c i", h=2, i=8)[:, h], ov, rv, ALU.mult, [bkb[6], rden], [aT])

        for g, tiles in enumerate(GROUPS):
            if g not in dbg.get("groups", (0, 1, 2)):
                continue
            tok0 = 384 * g
            alias([hT, sgT, aT], [actT]); alias([qT], [hmT])
            if g == 2:
                alias([ckb, cvb, kcT, PTc, qB], [mergedT, x1])
            if g == 0:
                load_norm_T(xprev[128 * (npre - 1):128 * npre, :], 0, hT, 0, gcolA)
                blk, bv = wget(w_in[:, 1024:1280].rearrange("(k p) c -> p k c", p=128), 128, (16, 256))
                k_proj(blk, bv, slice(0, 128), 128, [0])
                blk, bv = wget(w_in[:, 1280:1536].rearrange("(k p) c -> p k c", p=128), 128, (16, 256))
                v_proj(blk, bv, 0, 0)
            for tl, tile in enumerate(tiles):
                load_norm_T(xown[128 * tile:128 * (tile + 1), :], tl % 2, hT, 128 * tl, gcolA)
            stage("D1a", g)
            for qb_ in range(4):
                blk, bv = wget(w_in[:, 256 * qb_:256 * qb_ + 256].rearrange("(k p) c -> p k c", p=128), 128, (16, 256))
                for half in range(2):
                    chunk = 2 * qb_ + half
                    bk = obank()
                    for k in range(16):
                        mm(banks[bk][:, 0:384], bv[:, k, 128 * half:128 * half + 128], hT[:, k, :], k == 0, k == 15, [blk, hT], [bkb[bk]])
                    act(qT[:, chunk, :], banks[bk][:, 0:384], AF.Copy, [bkb[bk]], [qT], scale=0.125)
                    if g == 2:
                        bk = obank()
                        for k in range(16):
                            mm(banks[bk][64:128, 0:128], bv[:, k, 128 * half:128 * half + 64], hT[:, k, 256:384], k == 0, k == 15, [blk, hT], [bkb[bk]], tp=(0, 64))
                        for k in range(16):
                            mm(banks[bk][0:64, 0:128], bv[:, k, 128 * half + 64:128 * half + 128], hT[:, k, 256:384], k == 0, k == 15, [blk, hT], [bkb[bk]], tp=(0, 0))
                        act(qB[:, chunk, :], banks[bk][:, 0:128], AF.Copy, [bkb[bk]], [qB], scale=0.125)
            stage("D1b", g)
            blk, bv = wget(w_in[:, 1024:1280].rearrange("(k p) c -> p k c", p=128), 128, (16, 256))
            k_proj(blk, bv, slice(0, 384), 384, [(t + 1) % 4 for t in tiles])
            stage("D1c", g)
            for tl, tile in enumerate(tiles):
                if tile >= 7:
                    bk = obank()
                    for k in range(16):
                        mm(banks[bk][:, 0:256], hT[:, k, 128 * tl:128 * tl + 128], bv[:, k, :], k == 0, k == 15, [blk, hT], [bkb[bk]])
                    cp(DVE, kvo[:, 0:256], banks[bk][:, 0:256], [bkb[bk]], [kvo])
                    if tile == 7:
                        dma(SP, kv_last[0], kvo[:, 0:256], [kvo], [], "kvo_out")
            blk, bv = wget(w_in[:, 1280:1536].rearrange("(k p) c -> p k c", p=128), 128, (16, 256))
            for tl, tile in enumerate(tiles):
                v_proj(blk, bv, 128 * tl, (tile + 1) % 4, 256 if tile >= 7 else None)
                if tile == 7:
                    dma(SP, kv_last[1], kvo[:, 256:512], [kvo], [], "kvo_out")
                if tile == 8 and not dbg.get("no_ksamp"):
                    dma(SP, ksamp[:, 120:128, :], kvo[:, 0:256], [kvo], [], "kvo_out")
                    dma(SP, vsamp[:, 120:128, :], kvo[:, 256:512], [kvo], [], "kvo_out")
            stage("D1d", g)
            for gb in range(16):
                blk, bv = wget(w_in[:, 2560 + 256 * gb:2560 + 256 * gb + 256].rearrange("(k p) c -> p k c", p=128), 128, (16, 256))
                for half in range(2):
                    chunk = 2 * gb + half
                    bk = obank()
                    for k in range(16):
                        mm(banks[bk][:, 0:384], bv[:, k, 128 * half:128 * half + 128], hT[:, k, :], k == 0, k == 15, [blk, hT], [bkb[bk]])
                    act(sgT[:, chunk, :], banks[bk][:, 0:384], AF.Sigmoid, [bkb[bk]], [sgT])
            stage("D1", g)
            for tl, tile in enumerate(tiles):
                col0 = 128 * tl
                if tile < 8:
                    for kv in range(4):
                        scores(kv, tile % 4, col0, 2, 3, Eprev, 0)
                        if tile == 0:
                            ts(DVE, PTm[0][:], PTm[0][:], padm[:], None, ALU.mult, None, [PTm[0], padm], [PTm[0]])
                        stage("D2a", g)
                        scores(kv, (tile + 1) % 4, col0, 4, 5, Ecur, 1)
                        stage("D2b", g)
                        for i, sl in enumerate((tile % 4, (tile + 1) % 4)):
                            mm(banks[6][0:64, :], Vtok[:, sl, 64 * kv:64 * kv + 64], PTm[i][:], i == 0, i == 1, [Vtok, PTm[i]], [bkb[6]])
                            mm(banks[7][0:64, :], ones_bf[:, 0:64], PTm[i][:], i == 0, i == 1, [ones_bf, PTm[i]], [bkb[7]])
                        stage("D2c", g)
                        attn_norm(kv, col0)
                        stage("D2d", g)
                else:
                    dma(POOL, ckb[:], cache_k.rearrange("b w c -> w b c"), [], [ckb], "ckb")
                    dma(POOL, cvb[:], cache_v.rearrange("b w c -> w b c"), [], [cvb], "cvb")
                    dma(SP, ksamp[:, 0:120, :], cache_k[:, 8:128, :], [cc_tok], [], "cache_copy")
                    dma(SP, vsamp[:, 0:120, :], cache_v[:, 8:128, :], [cc_tok], [], "cache_copy")
                    for b in range(16):
                        bk = b % 4
                        pst = banks[bk][:].bitcast(BF16)
                        if True:
                            for c2 in range(2):
                                tr(pst[:, 128 * c2:128 * c2 + 128], ckb[:, b, 128 * c2:128 * c2 + 128], ident[:], [ckb, ident], [bkb[bk]])
                            cp(ACT, kcT[:, b, :, :].rearrange("p c w -> p (c w)"), pst[:, 0:256], [bkb[bk]], [kcT])
                    for b in range(16):
                        for kv in range(4):
                            bk = (kv % 2) * 2 + b // 8
                            psc = banks[bk][:].rearrange("p (b k s i) -> p b k s i", b=8, k=2, s=4)
                            be = 64 * (kv % 2)
                            s_nat, s_swp = ((0, 2) if kv % 2 == 0 else (2, 0))
                            mm(psc[:, b % 8, kv // 2, s_nat:s_nat + 2, :].rearrange("p s i -> p (s i)"), kcT[be:be + 64, b, kv // 2, :], qT[be:be + 64, 2 * kv:2 * kv + 2, 256 + 8 * b:256 + 8 * b + 8],
                               True, True, [kcT, qT], [bkb[bk]], tp=(be, 0))
                            mm(psc[:, b % 8, kv // 2, s_swp:s_swp + 2, :].rearrange("p s i -> p (s i)"), kcT[be:be + 64, b, kv // 2, :], qB[be:be + 64, 2 * kv:2 * kv + 2, 8 * b:8 * b + 8],
                               True, True, [kcT, qB], [bkb[bk]], tp=(be, 0))
                    for bk in range(4):
                        act(PTc[:, 512 * bk:512 * bk + 512], banks[bk][:], AF.Exp, [bkb[bk]], [PTc])
                    for kvpar in range(2):
                        for bh in range(2):
                            for kvh in range(2):
                                kvv = 2 * kvh + kvpar
                                o_ = 1024 * kvpar + 512 * bh
                                v_ = PTc[:, o_:o_ + 512].rearrange("p (b k s i) -> p b k s i", b=8, k=2, s=4)[:, :, kvh]
                                tt(DVE, v_, v_, Ecache[:, None, 4 * kvv:4 * kvv + 4, :].broadcast_to([128, 8, 4, 8]), ALU.mult, [PTc, Ecache], [PTc])
                    for kv in range(4):
                        scores(kv, (tile + 1) % 4, col0, 4, 5, Enew, 1, perm=True)
                        mm(banks[6][0:64, :], Vtok[:, (tile + 1) % 4, 64 * kv:64 * kv + 64], PTp[:], True, False, [Vtok, PTp], [bkb[6]])
                        mm(banks[7][0:64, :], ones_bf[:, 0:64], PTp[:], True, False, [ones_bf, PTp], [bkb[7]])
                        for b in range(16):
                            o_ = 1024 * (kv % 2) + 512 * (b // 8)
                            pr = PTc[:, o_:o_ + 512].rearrange("p (b k x) -> p b k x", b=8, k=2)[:, b % 8, kv // 2, :]
                            mm(banks[6][0:64, 32 * b:32 * b + 32], cvb[:, b, 64 * kv:64 * kv + 64], pr, False, b == 15, [cvb, PTc], [bkb[6]])
                            mm(banks[7][0:64, 32 * b:32 * b + 32], ones_bf[:, 0:64], pr, False, b == 15, [ones_bf, PTc], [bkb[7]])
                        attn_norm_sample(kv, col0)
                    alias([mergedT, x1], [ckb, cvb, kcT, PTc, qB])
            stage("D2", g)
            for gb in range(4):
                blk, bv = wget(w_glu[:, 256 * gb:256 * gb + 256].rearrange("(k p) c -> p k c", p=128), 128, (8, 256))
                for half in range(2):
                    chunk = 2 * gb + half
                    bk = obank()
                    for k in range(8):
                        mm(banks[bk][:, 0:384], bv[:, k, 128 * half:128 * half + 128], zT[:, k, tok0:tok0 + 384], k == 0, k == 7, [blk, zT], [bkb[bk]])
                    tb_ = tmpb[chunk % 2]
                    act(tb_[:], banks[bk][:, 0:384], AF.Sigmoid, [bkb[bk], bglu], [tb_], bias=bglu[:, chunk:chunk + 1])
                    tt(DVE, sT[:, chunk, :], tb_[:], zT[:, chunk, tok0:tok0 + 384], ALU.mult, [tb_, zT], [sT])
            stage("D3", g)
            for cb in range(8):
                blkA, bvA = wget(w_ab[:, 256 * cb:256 * cb + 256].rearrange("(h p) c -> p h c", p=64), 64, (16, 256))
                blkS, bvS = wget(w_sb[:, 256 * cb:256 * cb + 256].rearrange("(k p) c -> p k c", p=128), 128, (8, 256))
                for half in range(2):
                    c = 2 * cb + half
                    bka = obank()
                    for h in range(16):
                        mm(banks[bka][:, 0:384], bvA[:, h, 128 * half:128 * half + 128], aT[0:64, h, :], h == 0, h == 15, [blkA, aT], [bkb[bka]])
                    bks = obank()
                    for k in range(8):
                        mm(banks[bks][:, 0:384], bvS[:, k, 128 * half:128 * half + 128], sT[:, k, :], k == 0, k == 7, [blkS, sT], [bkb[bks]])
                    tt(DVE, tmpf[0][:], banks[bka][:, 0:384], sgT[:, c, :], ALU.mult, [bkb[bka], sgT], [tmpf[0]])
                    tt(DVE, tmpf[1][:], banks[bks][:, 0:384], sgT[:, 16 + c, :], ALU.mult, [bkb[bks], sgT], [tmpf[1]])
                    tt(DVE, mergedT[:, c, :], tmpf[0][:], tmpf[1][:], ALU.add, [tmpf[0], tmpf[1]], [mergedT])
            stage("D4", g)
            for tl, tile in enumerate(tiles):
                dma(SP, x1[:, tl, :], xown[128 * tile:128 * (tile + 1), :], [], [x1], "x1in")
            for cb in range(4):
                blk0, bv0 = wget(w_out[0:1024, 512 * cb:512 * cb + 512].rearrange("(k p) c -> p k c", p=128), 128, (8, 512))
                blk1, bv1 = wget(w_out[1024:2048, 512 * cb:512 * cb + 512].rearrange("(k p) c -> p k c", p=128), 128, (8, 512))
                for tl in range(3):
                    bk = 5 + tl
                    for k in range(16):
                        bv_, blk_ = (bv0, blk0) if k < 8 else (bv1, blk1)
                        mm(banks[bk][:], mergedT[:, k, 128 * tl:128 * tl + 128], bv_[:, k % 8, :], k == 0, k == 15, [mergedT, blk_], [bkb[bk]])
                    tt(DVE, x1[:, tl, 512 * cb:512 * cb + 512], banks[bk][:], x1[:, tl, 512 * cb:512 * cb + 512], ALU.add, [bkb[bk], x1], [x1])
            stage("D5", g)
            alias([hmT], [qT])
            for tl in range(3):
                load_norm_T(None, tl % 2, hmT, 128 * tl, gcolM, fp32_src=x1[:, tl, :])
            stage("D6", g)
            alias([actT], [hT, sgT, aT])
            for ub in range(dbg.get("d7_ub", 32)):
                blk, bv = wget(w_up[:, 256 * ub:256 * ub + 256].rearrange("(k p) c -> p k c", p=128), 128, (16, 256))
                for half in range(2):
                    f = 2 * ub + half
                    bk = obank()
                    for k in range(16):
                        mm(banks[bk][:, 0:384], bv[:, k, 128 * half:128 * half + 128], hmT[:, k, :], k == 0, k == 15, [blk, hmT], [bkb[bk]])
                    tb_ = tmpb[f % 2]
                    act(tb_[:], banks[bk][:, 0:384], AF.Relu, [bkb[bk]], [tb_])
                    tt(DVE, actT[:, f, :], tb_[:], tb_[:], ALU.mult, [tb_], [actT])
            stage("D7", g)
            for cb in range(dbg.get("d8_cb", 4)):
                for rg in range(dbg.get("d8_rg", 8)):
                    blk, bv = wget(w_down[1024 * rg:1024 * rg + 1024, 512 * cb:512 * cb + 512].rearrange("(k p) c -> p k c", p=128), 128, (8, 512))
                    for tl in range(3):
                        for f8 in range(8):
                            mm(banks[5 + tl][:], actT[:, 8 * rg + f8, 128 * tl:128 * tl + 128], bv[:, f8, :], rg == 0 and f8 == 0, rg == dbg.get("d8_rg", 8) - 1 and f8 == 7,
                               [actT, blk], [bkb[5 + tl]])
                for tl in range(3):
                    tt(DVE, x1[:, tl, 512 * cb:512 * cb + 512], banks[5 + tl][:], x1[:, tl, 512 * cb:512 * cb + 512], ALU.add, [bkb[5 + tl], x1], [x1])
            stage("D8", g)
            for tl, tile in enumerate(tiles):
                s = tl % 2
                rms_stats(x1[:, tl, :], xnD[s][:], ssD[s], rsD[s], [x1], xnD[s])
                stt(DVE, x1[:, tl, :], x1[:, tl, :], rsD[s][:], gfin[:], ALU.mult, ALU.mult, [x1, rsD[s], gfin], [x1])
                dma(SP, y_own[128 * tile:128 * (tile + 1), :], x1[:, tl, :], [x1], [], "yout")
            stage("D9", g)

    P.op = lambda *a, **k: None
    P.dma = lambda *a, **k: None
    try:
        phaseD()
    except StopD:
        pass
    P.op, P.dma = realop, realdma
    P_plan[0] = False
    try:
        phaseD()
    except StopD:
        pass
    final_toks.extend([x1.b, kvo.b, cc_tok])
    print("sbuf bytes remaining", nc.sbuf_bytes_remaining)
    P.emit(final_bufs=final_toks)
    sD.close()
    es.close()
    return nc, P


def _prep_in_maps(I, npre=NPRE, ncores=NCORES):
    C = _host_consts()
    f32 = lambda a: np.ascontiguousarray(np.asarray(a, dtype=np.float32))
    xp = f32(I["x_prompt"])[0]
    full = np.concatenate([np.zeros((112, 2048), np.float32), f32(I["meta_tokens"]), xp], 0)
    xs = f32(I["x_sample"]).reshape(128 * 8, 2048)
    ck = f32(I["cache_k"]).reshape(128, 128, 256); cv = f32(I["cache_v"]).reshape(128, 128, 256)
    sr = f32(I["state_ssm_re"]).reshape(128, 4096); si = f32(I["state_ssm_im"]).reshape(128, 4096)
    shared = {k: f32(I[k]) for k in ("g_attn_norm", "g_mlp_norm", "g_final_norm", "w_in", "ssm_a_re", "ssm_a_im", "ssm_log_dt",
                                     "ssm_b_re", "ssm_b_im", "ssm_c_re", "ssm_c_im", "ssm_d", "w_glu", "b_glu", "w_attn_branch",
                                     "w_ssm_branch", "w_out", "w_up", "w_down")}
    shared["sinks_perm"] = f32(I["sinks"])[C["hperm"]]
    for k in ("ident_bf", "ident_f", "kcol", "maskc", "e_cur", "e_new", "e_cache"):
        shared[k] = C[k]
    maps = []
    for c in range(ncores):
        end = 128 + 1024 * c
        st = end - npre * 128
        seg = full[max(st, 0):end]
        if st < 0:
            seg = np.concatenate([np.zeros((-st, 2048), np.float32), seg], 0)
        m = dict(shared)
        m["xprev"] = np.ascontiguousarray(seg)
        m["xown"] = np.ascontiguousarray(np.concatenate([full[end:end + 1024], xs[128 * c:128 * (c + 1)]], 0))
        m["cache_k"] = np.ascontiguousarray(ck[16 * c:16 * (c + 1)]); m["cache_v"] = np.ascontiguousarray(cv[16 * c:16 * (c + 1)])
        m["st_re"] = np.ascontiguousarray(sr[16 * c:16 * (c + 1)]); m["st_im"] = np.ascontiguousarray(si[16 * c:16 * (c + 1)])
        m["e_prev"] = C["e_prev"]
        m["padmask"] = (np.arange(128) >= 112).astype(np.float32).reshape(128, 1) if c == 0 else np.ones((128, 1), np.float32)
        maps.append(m)
    return maps


_CACHE = {}


def kernel(**inputs):
    if "prog" not in _CACHE:
        _CACHE["prog"] = build_program()[0]
    nc = _CACHE["prog"]
    maps = _prep_in_maps(inputs)
    res = run_bass_kernel_spmd(nc, maps, core_ids=list(range(NCORES)))
    R = res.results
    y_prompt = np.concatenate([R[c]["y_own"][:1024] for c in range(NCORES)], 0).reshape(1, 8192, 2048)
    y_sample = np.concatenate([R[c]["y_own"][1024:] for c in range(NCORES)], 0).reshape(128, 8, 2048)
    k_prompt = R[7]["kv_last"][0].reshape(1, 128, 4, 64)
    v_prompt = R[7]["kv_last"][1].reshape(1, 128, 4, 64)

    def fromH(h):
        h = h.reshape(2, 64, 2, 32)
        return (np.ascontiguousarray(h[:, :, 0, :].transpose(2, 0, 1)).reshape(1, 64, 64),
                np.ascontiguousarray(h[:, :, 1, :].transpose(2, 0, 1)).reshape(1, 64, 64))
    re_p, im_p = fromH(R[7]["ssm_fin"])
    k_sample = np.concatenate([R[c]["ksamp"] for c in range(NCORES)], 0).reshape(128, 128, 4, 64)
    v_sample = np.concatenate([R[c]["vsamp"] for c in range(NCORES)], 0).reshape(128, 128, 4, 64)
    ss = np.concatenate([R[c]["ssm_samp"] for c in range(NCORES)], 0)
    re_s = np.ascontiguousarray(ss[:, 0]).reshape(128, 64, 64)
    im_s = np.ascontiguousarray(ss[:, 1]).reshape(128, 64, 64)
    f = lambda a: np.ascontiguousarray(a, dtype=np.float32)
    return (f(y_prompt), f(y_sample), f(k_prompt), f(v_prompt), f(re_p), f(im_p), f(k_sample), f(v_sample), f(re_s), f(im_s))
```
